# Optimizing a Trainium2 kernel written in Bass

```python
import functools
import jax, jax.numpy as jnp
from jax import lax
import numpy as np

D_MODEL = 1024
BATCH = 16
SEQ = 256
DEPTH = 2
DEC_BATCH = 4
DEC_SEQ = 2048
PAST_LEN = 256

GRID_W = 64
HEAD_DIM = 64
WIDTH_A = D_MODEL // 2
N_HEADS_A = WIDTH_A // HEAD_DIM
WIN_R = 8
WIN_C = 16
WIDTH_B = D_MODEL // 4
N_GROUPS_B = 4
GROUP_B = WIDTH_B // N_GROUPS_B
WIDTH_C = D_MODEL // 4
POOL_WINDOWS = (2, 4, 8, 16)
GROUP_C = WIDTH_C // len(POOL_WINDOWS)
Q_BLOCK = 128
EPS = 1e-6
NEG_INF = -1e30
BASE_W = 4 * WIDTH_A + 2 * WIDTH_B + 2 * WIDTH_C
IN_WIDTH = BASE_W + 3 * D_MODEL
SPLIT_POINTS = (WIDTH_A, 2 * WIDTH_A, 3 * WIDTH_A, 4 * WIDTH_A,
                4 * WIDTH_A + WIDTH_B, 4 * WIDTH_A + 2 * WIDTH_B,
                4 * WIDTH_A + 2 * WIDTH_B + WIDTH_C, BASE_W,
                BASE_W + D_MODEL, BASE_W + 2 * D_MODEL)

kernel_name = 'hybrid_natten_fnet_pool_diffusion_step'


def _rmsnorm(x, g):
    xf = x.astype(jnp.float32)
    y = xf * lax.rsqrt(jnp.mean(xf * xf, axis=-1, keepdims=True) + EPS)
    return y.astype(x.dtype) * g


def _ada(cond, w_ada, b_ada):
    a = jax.nn.silu(cond) @ w_ada + b_ada
    return jnp.split(a, 3, axis=-1)


def _heads(t):
    b, l, _ = t.shape
    return t.reshape(b, l, N_HEADS_A, HEAD_DIM).transpose(0, 2, 1, 3)


def _ctx_attention(q, k, v):
    b, h, l, d = q.shape
    nb = l // Q_BLOCK
    qb = q.reshape(b, h, nb, Q_BLOCK, d).transpose(2, 0, 1, 3, 4)
    scale = HEAD_DIM ** -0.5

    def block(qi):
        s = jnp.einsum('bhqd,bhkd->bhqk', qi, k).astype(jnp.float32) * scale
        p = jax.nn.softmax(s, axis=-1).astype(v.dtype)
        return jnp.einsum('bhqk,bhkd->bhqd', p, v)

    o = lax.map(block, qb)
    return o.transpose(1, 0, 3, 2, 4).reshape(b, l, h * d)


def _na_attention(q, k, v, k_ctx, v_ctx, rpb):
    b, h, n, d = q.shape
    rows = n // GRID_W
    wr = min(WIN_R, rows)
    nw = wr * GRID_W
    kg = k.reshape(b, h, rows, GRID_W, d)
    vg = v.reshape(b, h, rows, GRID_W, d)
    q_rows = q.reshape(b, h, rows, GRID_W, d).transpose(2, 0, 1, 3, 4)
    row_start = jnp.clip(jnp.arange(rows) - wr // 2, 0, rows - wr)
    col = jnp.arange(GRID_W)
    col_start = jnp.clip(col - WIN_C // 2, 0, GRID_W - WIN_C)
    col_ok = (col[None, :] >= col_start[:, None]) & (col[None, :] < col_start[:, None] + WIN_C)
    win_mask = jnp.broadcast_to(col_ok[:, None, :], (GRID_W, wr, GRID_W)).reshape(GRID_W, nw)
    col_idx = jnp.clip(col[None, :] - col[:, None] + WIN_C - 1, 0, 2 * WIN_C - 2)
    scale = HEAD_DIM ** -0.5

    def row_block(args):
        qr, r = args
        rs = row_start[r]
        kw = lax.dynamic_slice_in_dim(kg, rs, wr, axis=2).reshape(b, h, nw, d)
        vw = lax.dynamic_slice_in_dim(vg, rs, wr, axis=2).reshape(b, h, nw, d)
        row_idx = rs + jnp.arange(wr) - r + WIN_R - 1
        bias = rpb[:, row_idx[None, :, None], col_idx[:, None, :]].reshape(h, GRID_W, nw)
        s_win = jnp.einsum('bhqd,bhkd->bhqk', qr, kw).astype(jnp.float32) * scale + bias.astype(jnp.float32)
        s_win = jnp.where(win_mask, s_win, NEG_INF)
        s_ctx = jnp.einsum('bhqd,bhcd->bhqc', qr, k_ctx).astype(jnp.float32) * scale
        p = jax.nn.softmax(jnp.concatenate([s_win, s_ctx], axis=-1), axis=-1).astype(v.dtype)
        return (jnp.einsum('bhqk,bhkd->bhqd', p[..., :nw], vw)
                + jnp.einsum('bhqc,bhcd->bhqd', p[..., nw:], v_ctx))

    o = lax.map(row_block, (q_rows, jnp.arange(rows)))
    return o.transpose(1, 0, 3, 2, 4).reshape(b, n, h * d)


def _fourier(u, w_fnet):
    b, l, _ = u.shape
    ug = u.reshape(b, l, N_GROUPS_B, GROUP_B).astype(jnp.float32)
    f = jnp.fft.fft2(ug, axes=(1, 3), norm='ortho').real
    return f.reshape(b, l, WIDTH_B).astype(u.dtype) @ w_fnet


def _pool(u, w_pool, pool_scale):
    b, l, _ = u.shape
    uf = u.astype(jnp.float32)
    csum = jnp.concatenate([jnp.zeros((b, 1, WIDTH_C), jnp.float32), jnp.cumsum(uf, axis=1)], axis=1)
    t = jnp.arange(l)
    outs = []
    for gi, w in enumerate(POOL_WINDOWS):
        lo = jnp.clip(t - w // 2, 0, l)
        hi = jnp.clip(t + w // 2, 0, l)
        cg = csum[:, :, gi * GROUP_C:(gi + 1) * GROUP_C]
        mean = (cg[:, hi] - cg[:, lo]) / (hi - lo).astype(jnp.float32)[None, :, None]
        outs.append(mean - uf[:, :, gi * GROUP_C:(gi + 1) * GROUP_C])
    dlt = jnp.stack(outs, axis=2).astype(u.dtype)
    y = jnp.einsum('blgc,gce->blge', dlt, w_pool).reshape(b, l, WIDTH_C)
    return y * pool_scale


def _layer(x, shift, scale, gate, lp, attend):
    (norm_g, w_in, q_g, k_g, w_fnet, w_pool, pool_scale, p_a, p_b, p_c, w_o) = lp
    h = _rmsnorm(x, norm_g) * (1 + scale) + shift
    q, k, v, z_a, u_b, z_b, u_c, z_c, g_a, g_b, g_c = jnp.split(h @ w_in, SPLIT_POINTS, axis=-1)
    q = _rmsnorm(_heads(q), q_g)
    k = _rmsnorm(_heads(k), k_g)
    v = _heads(v)
    y_a = attend(q, k, v)
    y_b = _fourier(u_b, w_fnet)
    y_c = _pool(u_c, w_pool, pool_scale)
    merged = (jax.nn.sigmoid(g_a) * ((jax.nn.silu(z_a) * y_a) @ p_a)
              + jax.nn.sigmoid(g_b) * ((jax.nn.silu(z_b) * y_b) @ p_b)
              + jax.nn.sigmoid(g_c) * ((jax.nn.silu(z_c) * y_c) @ p_c))
    return x + gate * (merged @ w_o), k, v


def setup_inputs(seed: int = 0) -> dict:
    key = jax.random.key(seed)
    ks = jax.random.split(key, 20)

    def nrm(k, shape, s):
        return jax.random.normal(k, shape, jnp.float32) * s

    return {
        'x_prompt': nrm(ks[0], (BATCH, SEQ, D_MODEL), 1.0),
        'x_sample': nrm(ks[1], (DEC_BATCH, DEC_SEQ, D_MODEL), 1.0),
        'cache_k': nrm(ks[2], (DEC_BATCH, DEPTH, N_HEADS_A, PAST_LEN, HEAD_DIM), 1.0),
        'cache_v': nrm(ks[3], (DEC_BATCH, DEPTH, N_HEADS_A, PAST_LEN, HEAD_DIM), 1.0),
        'c': nrm(ks[4], (DEC_BATCH, D_MODEL), 1.0),
        'c_ctx': nrm(ks[5], (D_MODEL,), 1.0),
        'norm_g': 1.0 + nrm(ks[6], (DEPTH, D_MODEL), 0.02),
        'w_ada': nrm(ks[7], (DEPTH, D_MODEL, 3 * D_MODEL), 0.5 * D_MODEL ** -0.5),
        'b_ada': nrm(ks[8], (DEPTH, 3 * D_MODEL), 0.01),
        'w_in': nrm(ks[9], (DEPTH, D_MODEL, IN_WIDTH), D_MODEL ** -0.5),
        'q_norm_g': 1.0 + nrm(ks[10], (DEPTH, HEAD_DIM), 0.02),
        'k_norm_g': 1.0 + nrm(ks[11], (DEPTH, HEAD_DIM), 0.02),
        'rpb': nrm(ks[12], (DEPTH, N_HEADS_A, 2 * WIN_R - 1, 2 * WIN_C - 1), 0.1),
        'w_fnet': nrm(ks[13], (DEPTH, WIDTH_B, WIDTH_B), WIDTH_B ** -0.5),
        'w_pool': nrm(ks[14], (DEPTH, len(POOL_WINDOWS), GROUP_C, GROUP_C), GROUP_C ** -0.5),
        'pool_scale': 1.0 + nrm(ks[15], (DEPTH, WIDTH_C), 0.02),
        'p_a': nrm(ks[16], (DEPTH, WIDTH_A, D_MODEL), WIDTH_A ** -0.5),
        'p_b': nrm(ks[17], (DEPTH, WIDTH_B, D_MODEL), WIDTH_B ** -0.5),
        'p_c': nrm(ks[18], (DEPTH, WIDTH_C, D_MODEL), WIDTH_C ** -0.5),
        'w_o': nrm(ks[19], (DEPTH, D_MODEL, D_MODEL), D_MODEL ** -0.5),
    }


def reference(x_prompt, x_sample, cache_k, cache_v, c, c_ctx, norm_g, w_ada, b_ada, w_in,
              q_norm_g, k_norm_g, rpb, w_fnet, w_pool, pool_scale, p_a, p_b, p_c, w_o):
    xp = x_prompt
    xs = x_sample
    new_k = []
    new_v = []
    for l in range(DEPTH):
        lp = (norm_g[l], w_in[l], q_norm_g[l], k_norm_g[l], w_fnet[l], w_pool[l],
              pool_scale[l], p_a[l], p_b[l], p_c[l], w_o[l])
        shift, scale, gate = _ada(c_ctx, w_ada[l], b_ada[l])
        xp, k_l, v_l = _layer(xp, shift, scale, gate, lp, _ctx_attention)
        new_k.append(k_l)
        new_v.append(v_l)
        shift, scale, gate = _ada(c[:, None, :], w_ada[l], b_ada[l])
        attend = functools.partial(_na_attention, k_ctx=cache_k[:, l], v_ctx=cache_v[:, l], rpb=rpb[l])
        xs, _, _ = _layer(xs, shift, scale, gate, lp, attend)
    return (xp, xs, jnp.stack(new_k, axis=1), jnp.stack(new_v, axis=1))
```

```python
import numpy as np
import ml_dtypes
from contextlib import ExitStack
import concourse.bass as bass
import concourse.mybir as mybir
from concourse.bass_utils import run_bass_kernel_spmd

F32 = mybir.dt.float32
BF16 = mybir.dt.bfloat16
ALU = mybir.AluOpType
AF = mybir.ActivationFunctionType
AX = mybir.AxisListType
NPBF = ml_dtypes.bfloat16

D = 1024
NT = 12
NTOK = NT * 128
EPS = 1e-6
NEG = -1e30
NBLK = 50


class Res:
    __slots__ = ("name", "w", "r")

    def __init__(self, name):
        self.name = name
        self.w = []
        self.r = []


class Prog:
    ENGS = ("pe", "act", "dve", "pool", "sp")
    NDMA = 8

    def __init__(self):
        self.ops = {e: [] for e in self.ENGS}
        self.dma_cnt = {e: 0 for e in self.ENGS}

    def op(self, eng, fn, rd=(), wr=(), kind="c", inc=None):
        ops = self.ops[eng]
        idx = len(ops)
        deps = []
        for r in rd:
            deps.extend(r.w)
        for r in wr:
            deps.extend(r.w)
            deps.extend(r.r)
        tok = (eng, idx, kind)
        rec = dict(fn=fn, deps=deps, kind=kind, signal=False, inc=inc)
        if kind != "c":
            rec["dq"] = self.dma_cnt[eng]
            self.dma_cnt[eng] += 1
        ops.append(rec)
        for r in wr:
            r.w = [tok]
            r.r = []
        for r in rd:
            if r not in wr:
                r.r.append(tok)
        return tok

    def emit(self, nc):
        ops = self.ops
        for e in self.ENGS:
            for rec in ops[e]:
                for (de, di, dk) in rec["deps"]:
                    if dk == "c" and not (de == "pe" and e == "pe"):
                        ops[de][di]["signal"] = True
        for e in self.ENGS:
            c = 0
            for rec in ops[e]:
                if rec["kind"] == "c" and rec["signal"]:
                    c += 1
                rec["sigval"] = c
        with ExitStack() as st:
            csem = {e: st.enter_context(nc.semaphore("c_" + e)) for e in self.ENGS}
            dsem = {e: [st.enter_context(nc.semaphore("d_%s%d" % (e, i))) for i in range(self.NDMA)]
                    for e in self.ENGS if self.dma_cnt[e] > 0}
            block = st.enter_context(nc.Block())
            for e in self.ENGS:
                per = [0] * self.NDMA
                for rec in ops[e]:
                    if rec["kind"] != "c":
                        s = rec["dq"] % self.NDMA
                        step = 16 if rec["inc"] is None else rec["inc"]
                        rec["dprev"] = per[s]
                        per[s] += step
                        rec["dval"] = per[s]
                        rec["dstep"] = step
                        rec["dsem"] = dsem[e][s]

            def run(e, eng):
                waited = {}
                for rec in ops[e]:
                    need = {}
                    for (de, di, dk) in rec["deps"]:
                        drec = ops[de][di]
                        if dk == "c":
                            if de == "pe" and e == "pe":
                                continue
                            key = ("c", de)
                            sem = csem[de]
                            val = drec["sigval"]
                        else:
                            sem = drec["dsem"]
                            key = ("d", id(sem))
                            val = drec["dval"]
                        if val > need.get(key, (None, 0))[1]:
                            need[key] = (sem, val)
                    if rec["kind"] != "c" and rec["dprev"] > 0:
                        sem = rec["dsem"]
                        key = ("d", id(sem))
                        if rec["dprev"] > need.get(key, (None, 0))[1]:
                            need[key] = (sem, rec["dprev"])
                    for key, (sem, val) in need.items():
                        if waited.get(key, 0) < val:
                            eng.wait_ge(sem, val)
                            waited[key] = val
                    ins = rec["fn"](eng)
                    if rec["kind"] == "c":
                        if rec["signal"]:
                            ins.then_inc(csem[e], 1)
                    else:
                        ins.then_inc(rec["dsem"], rec["dstep"])
                if e in dsem:
                    per = {}
                    for rec in ops[e]:
                        if rec["kind"] != "c":
                            per[id(rec["dsem"])] = (rec["dsem"], rec["dval"])
                    for sem, val in per.values():
                        eng.wait_ge(sem, val)

            if ops["pe"]:
                @block.tensor
                def _(eng):
                    run("pe", eng)
            if ops["act"]:
                @block.scalar
                def _(eng):
                    run("act", eng)
            if ops["dve"]:
                @block.vector
                def _(eng):
                    run("dve", eng)
            if ops["pool"]:
                @block.gpsimd
                def _(eng):
                    run("pool", eng)
            if ops["sp"]:
                @block.sync
                def _(eng):
                    run("sp", eng)


def _na_schedule():
    sched = []
    for qh in range(2):
        lo, hi = qh * 8, qh * 8 + 7
        items = []
        off = 0
        for j in range(8):
            p0 = 0 if j <= 3 else 2 * j - 4
            p1 = min(2 * j + 5, 15)
            a, b = max(p0, lo), min(p1, hi)
            if a <= b:
                items.append(("own", j, a, b - a + 1, off))
                off += b - a + 1
        if qh == 1:
            for hidx in range(4):
                items.append(("halo", hidx, 12, 4, off))
                off += 4
        assert off <= NBLK
        sched.append(items)
    return sched


NA_SCHED = _na_schedule()


def _pool_slots(i):
    if i < 4:
        s = 2 * (i // 2)
        return [("own", s), ("own", s + 1)]
    m = i - 4
    sl = []
    if m - 1 >= 0:
        sl.append(("own", 4 + m - 1))
    sl.append(("own", 4 + m))
    if m + 1 <= 7:
        sl.append(("own", 4 + m + 1))
    if m == 7:
        sl.append(("edge", 0))
        sl.append(("edge", 1))
    return sl


def build_nc():
    nc = bass.Bass("TRN2", target_bir_lowering=False)

    def din(name, shape, dt=F32):
        return nc.dram_tensor(name, list(shape), dt, kind="ExternalInput").ap()

    def dout(name, shape, dt=F32):
        return nc.dram_tensor(name, list(shape), dt, kind="ExternalOutput").ap()

    xin = din("xin", [NTOK, D])
    condT = din("condT", [128, 16])
    w_ada = din("w_ada", [2, D, 3 * D])
    b_adaT = din("b_adaT", [2, 128, 24])
    norm_gT = din("norm_gT", [2, 128, 8])
    w_in = din("w_in", [2, D, 6 * D])
    gqT = din("gqT", [2, 128, 1])
    k_norm_g = din("k_norm_g", [2, 64])
    cache_k = din("cache_k", [2, 8, 256, 64])
    cache_v = din("cache_v", [2, 8, 256, 64])
    w_fnet = din("w_fnet", [2, 256, 256])
    w_poolbd = din("w_poolbd", [2, 2, 128, 128])
    pool_scaleT = din("pool_scaleT", [2, 128, 2])
    p_a = din("p_a", [2, 512, D])
    p_b = din("p_b", [2, 256, D])
    p_c = din("p_c", [2, 256, D])
    w_o = din("w_o", [2, D, D])
    ident_d = din("ident", [128, 128], BF16)
    identf_d = din("identf", [128, 128])
    tt_d = din("tt", [2, 8, 2, 128, NBLK * 64], BF16)
    dfts_d = din("dfts", [16, 128, 2, 1024], BF16)
    dftp_d = din("dftp", [128, 2, 2, 256], BF16)
    cs_d = din("cs", [128, 256], BF16)
    bnd_d = din("bnd", [NT, 128, 4, 4, 128], BF16)

    yout = dout("yout", [NTOK, D])
    newk = dout("newk", [2, 2, 8, 256, 64])
    newv = dout("newv", [2, 2, 8, 256, 64])

    x1 = nc.dram_tensor("x1", [NTOK, D], F32).ap()
    cc_in = [nc.dram_tensor("cc_in%d" % l, [1664, 512], BF16).ap() for l in range(2)]
    cc_out = [nc.dram_tensor("cc_out%d" % l, [3328, 512], BF16).ap() for l in range(2)]

    P = Prog()
    st = ExitStack()

    def sb(name, shape, dt):
        return st.enter_context(nc.sbuf_tensor("s_" + name, list(shape), dt))

    hT = sb("hT", [128, 8, NTOK], BF16)
    A2 = sb("A2", [128, 12288], BF16)
    A3 = sb("A3", [128, 12288], BF16)
    A4 = sb("A4", [128, 12288], BF16)
    QT2 = A2[:, 0:6144].rearrange("p (j t) -> p j t", j=4)
    KT2 = A2[:, 6144:12288].rearrange("p (j t) -> p j t", j=4)
    UCUS = A2[:, 0:10240].rearrange("p (g n) -> p g n", g=20)
    MRG = A2[:, :].rearrange("p (t n) -> p t n", t=NT)
    zaT = A3[:, 0:6144].rearrange("p (j t) -> p j t", j=4)
    zbT = A3[:, 6144:9216].rearrange("p (j t) -> p j t", j=2)
    zcT = A3[:, 9216:12288].rearrange("p (j t) -> p j t", j=2)
    MT = A3[:, :].rearrange("p (k t) -> p k t", k=8)
    V = A4[:, 0:6144].rearrange("p (t n) -> p t n", t=NT)
    ubT = A4[:, 6144:9216].rearrange("p (j t) -> p j t", j=2)
    UC = A4[:, 9216:12288].rearrange("p (t n) -> p t n", t=NT)

    wbuf = [sb("wbuf%d" % i, [128, 8, 512], BF16) for i in range(3)]
    pbuf = sb("pbuf", [128, 8, 512], BF16)
    ttb = [sb("ttb%d" % i, [128, NBLK * 64], BF16) for i in range(1)]
    dslab = [sb("dslab%d" % i, [128, 2, 1024], BF16) for i in range(2)]
    dftp = sb("dftp", [128, 2, 2, 256], BF16)
    cs = sb("cs", [128, 256], BF16)
    ident = sb("identb", [128, 128], BF16)
    identf = sb("identfs", [128, 128], F32)
    onesb = sb("onesb", [128, 128], BF16)
    onesf = sb("onesf", [128, 128], F32)
    repf = [sb("repf%d" % i, [128, 128], F32) for i in range(2)]
    KTc2 = sb("KTc2", [128, 4, 256], BF16)
    Vctx = sb("Vctx", [128, 2, 512], BF16)
    Kctx = sb("Kctx", [128, 2, 512], BF16)
    KTh2 = sb("KTh2", [128, 4, 512], BF16)
    Khalo = sb("Khalo", [128, 4, 512], BF16)
    Vh = sb("Vh", [128, 4, 512], BF16)
    ucE = sb("ucE", [128, 2, 256], BF16)
    bndb = sb("bndb", [128, 4, 4, 128], BF16)
    gate_bc = [sb("gate_bc%d" % i, [128, D], F32) for i in range(2)]
    xt = [sb("xt%d" % i, [128, D], F32) for i in range(2)]
    hb = sb("hb", [128, D], BF16)
    Ft = [sb("Ft%d" % i, [128, 512], F32) for i in range(4)]
    Bt = [sb("Bt%d" % i, [128, 512], BF16) for i in range(4)]
    sq = sb("sq", [128, 512], F32)
    gkb = sb("gkb", [128, 64], F32)
    wfn = sb("wfn", [128, 2, 256], BF16)
    wpl = sb("wpl", [128, 2, 128], BF16)
    scol = sb("scol", [128, 16], F32)
    scb = sb("scb", [128, 16], BF16)
    modc = sb("modc", [128, 48], F32)
    g1c = sb("g1c", [128, 16], F32)
    badaT = sb("badaT", [128, 24], F32)
    ngT = sb("ngT", [128, 8], F32)
    gq8 = sb("gq8", [128, 1], F32)
    pscT = sb("pscT", [128, 2], F32)
    ssq = sb("ssq", [128, 2 * NT], F32)
    rstd = sb("rstd", [128, 2 * NT], F32)
    st8 = [sb("st8_%d" % i, [128, 8], F32) for i in range(2)]

    ps = [st.enter_context(nc.psum_tensor("ps%d" % i, [128, 512], F32)) for i in range(6)]
    pst = [st.enter_context(nc.psum_tensor("pst%d" % i, [128, 1024], BF16)) for i in range(2)]

    R = {}

    def res(name):
        if name not in R:
            R[name] = Res(name)
        return R[name]

    r_ps = [res("ps%d" % i) for i in range(6)]
    r_pst = [res("pst%d" % i) for i in range(2)]
    r_wbuf = [res("wbuf%d" % i) for i in range(3)]
    r_Ft = [res("Ft%d" % i) for i in range(4)]
    r_Bt = [res("Bt%d" % i) for i in range(4)]
    r_xt = [res("xt%d" % i) for i in range(2)]
    rot = {"ps": 0, "pst": 0, "w": 0, "F": 0, "B": 0, "xt": 0, "ev": 0, "st8": 0, "tt": 0, "ds": 0, "rep": 0}

    def nxt(kind, n):
        v = rot[kind]
        rot[kind] = (v + 1) % n
        return v

    def rt(buf, tiles):
        return [res("%s_%d" % (buf, t)) for t in tiles]

    def dma(q, out, in_, rd=(), wr=()):
        return P.op(q, lambda e: e.dma_start(out=out, in_=in_), rd=rd, wr=wr, kind="d")

    def mm(out, lhsT, rhs, start, stop, rd=(), wr=()):
        return P.op("pe", lambda e: e.matmul(out, lhsT=lhsT, rhs=rhs, start=start, stop=stop), rd=rd, wr=wr)

    def tr(out, in_, rd=(), wr=()):
        return P.op("pe", lambda e: e.transpose(out, in_, ident[:]), rd=list(rd) + [res("ident")], wr=wr)

    def act(out, in_, func, rd=(), wr=(), **kw):
        return P.op("act", lambda e: e.activation(out=out, in_=in_, func=func, **kw), rd=rd, wr=wr)

    def tt(eng, out, in0, in1, op, rd=(), wr=()):
        return P.op(eng, lambda e: e.tensor_tensor(out=out, in0=in0, in1=in1, op=op), rd=rd, wr=wr)

    def ts(eng, out, in0, s1, s2, op0, op1=None, rd=(), wr=()):
        if op1 is None:
            return P.op(eng, lambda e: e.tensor_scalar(out=out, in0=in0, scalar1=s1, scalar2=None, op0=op0), rd=rd, wr=wr)
        return P.op(eng, lambda e: e.tensor_scalar(out=out, in0=in0, scalar1=s1, scalar2=s2, op0=op0, op1=op1), rd=rd, wr=wr)

    def cp(eng, out, in_, rd=(), wr=()):
        if eng == "act":
            return act(out, in_, AF.Copy, rd=rd, wr=wr)
        return P.op(eng, lambda e: e.tensor_copy(out=out, in_=in_), rd=rd, wr=wr)

    def evac(out, in_, rd=(), wr=()):
        eng = "act" if nxt("ev", 2) == 0 else "dve"
        return cp(eng, out, in_, rd=rd, wr=wr)

    def memset(eng, ap, val, wr=()):
        return P.op(eng, lambda e: e.memset(ap, val), wr=wr)

    dma("sp", ident[:], ident_d, wr=[res("ident")])
    dma("sp", identf[:], identf_d, wr=[res("identf")])
    dma("sp", dftp[:], dftp_d, wr=[res("dftp")])
    dma("sp", cs[:], cs_d, wr=[res("cs")])
    dma("sp", scol[:], condT, wr=[res("scol")])
    memset("dve", onesb[:], 1.0, wr=[res("onesb")])
    memset("dve", onesf[:], 1.0, wr=[res("onesf")])
    act(scol[:], scol[:], AF.Silu, rd=[res("scol")], wr=[res("scol")])
    cp("dve", scb[:], scol[:], rd=[res("scol")], wr=[res("scb")])
    memset("dve", ssq[:], 0.0, wr=[res("ssq")])
    ccn = {"n": 0}

    def ccres(l):
        ccn["n"] += 1
        return res("ccin%d_%d" % (l, ccn["n"]))
    ccin_all = {0: [], 1: []}

    def cc_dma(l, dst, src, rd):
        r = ccres(l)
        ccin_all[l].append(r)
        dma("sp", dst, src, rd=rd, wr=[r])

    def wload(dst_ap, src_ap, slot):
        return dma("pool", dst_ap, src_ap, wr=[r_wbuf[slot]])

    def win_view(l, c0, n):
        return w_in[l].rearrange("(kc p) n -> p kc n", p=128)[:, :, c0:c0 + n]

    def transposes_to(dst_fn, src_tile, nchunks, rd, wr, scale_ap=None, bias_fn=None, scale_fn=None):
        b = nxt("pst", 2)
        for c in range(nchunks):
            tr(pst[b][:, c * 128:(c + 1) * 128], src_tile[:, c * 128:(c + 1) * 128], rd=rd, wr=[r_pst[b]])
        if scale_ap is None and scale_fn is None:
            for c in range(nchunks):
                evac(dst_fn(c), pst[b][:, c * 128:(c + 1) * 128], rd=[r_pst[b]], wr=wr)
        elif scale_fn is not None:
            for c in range(nchunks):
                act(dst_fn(c), pst[b][:, c * 128:(c + 1) * 128], AF.Identity, rd=[r_pst[b]] + scale_fn(c)[2], wr=wr,
                    scale=scale_fn(c)[0], bias=scale_fn(c)[1])
        else:
            for c in range(nchunks):
                act(dst_fn(c), pst[b][:, c * 128:(c + 1) * 128], AF.Identity, rd=[r_pst[b]] + [res("gq8")], wr=wr,
                    scale=scale_ap, bias=0.0)

    for l in range(2):
        xsrc = xin if l == 0 else x1
        xdst = x1 if l == 0 else yout
        L = "L%d" % l

        dma("sp", badaT[:], b_adaT[l], wr=[res("badaT")])
        dma("sp", ngT[:], norm_gT[l], wr=[res("ngT")])
        dma("sp", gq8[:], gqT[l], wr=[res("gq8")])
        dma("sp", gkb[:], k_norm_g[l:l + 1, :].partition_broadcast(128), wr=[res("gkb")])
        dma("sp", pscT[:], pool_scaleT[l], wr=[res("pscT")])
        dma("pool", wfn[:], w_fnet[l].rearrange("(kc p) n -> p kc n", p=128), wr=[res("wfn")])
        dma("pool", wpl[:], w_poolbd[l].rearrange("c p n -> p c n"), wr=[res("wpl")])
        act(gq8[:], gq8[:], AF.Copy, rd=[res("gq8")], wr=[res("gq8")], scale=0.125)
        for kc in range(8):
            s = nxt("w", 3)
            pm = nxt("ps", 4)
            wv = wbuf[s][:, :, :].rearrange("p a b -> p (a b)")[:, 0:3072]
            wload(wv, w_ada[l, kc * 128:(kc + 1) * 128, :], s)
            for n in range(24):
                mm(ps[pm][:, n * 2:(n + 1) * 2], wv[:, n * 128:(n + 1) * 128], scb[:, kc * 2:(kc + 1) * 2],
                   start=True, stop=True, rd=[r_wbuf[s], res("scb")], wr=[r_ps[pm]])
            if kc == 0:
                tt("dve", modc[:].rearrange("p (n c) -> p n c", c=2), ps[pm][:, 0:48].rearrange("p (n c) -> p n c", c=2),
                   badaT[:].unsqueeze(2).to_broadcast([128, 24, 2]), ALU.add, rd=[r_ps[pm], res("badaT")], wr=[res("modc")])
            else:
                tt("dve", modc[:], ps[pm][:, 0:48], modc[:], ALU.add, rd=[r_ps[pm], res("modc")], wr=[res("modc")])
        ts("dve", g1c[:], modc[:, 16:32], 1.0, None, ALU.add, rd=[res("modc")], wr=[res("g1c")])
        tt("dve", g1c[:].rearrange("p (n c) -> p n c", c=2), g1c[:].rearrange("p (n c) -> p n c", c=2),
           ngT[:].unsqueeze(2).to_broadcast([128, 8, 2]), ALU.mult, rd=[res("g1c"), res("ngT")], wr=[res("g1c")])
        for cond in range(2):
            for half in range(2):
                b = nxt("ps", 4)
                for c4 in range(4):
                    n = 16 + half * 4 + c4
                    rp = nxt("rep", 2)
                    ts("dve", repf[rp][:], onesf[:], modc[:, n * 2 + cond:n * 2 + cond + 1], None, ALU.mult,
                       rd=[res("onesf"), res("modc")], wr=[res("rep%d" % rp)])
                    mm(ps[b][:, c4 * 128:(c4 + 1) * 128], repf[rp][:], identf[:], True, True,
                       rd=[res("rep%d" % rp), res("identf")], wr=[r_ps[b]])
                evac(gate_bc[cond][:, half * 512:(half + 1) * 512], ps[b][:], rd=[r_ps[b]], wr=[res("gate_bc%d" % cond)])

        for t in range(NT):
            cond = 0 if t < 4 else 1
            xi = nxt("xt", 2)
            dma("sp", xt[xi][:], xsrc[t * 128:(t + 1) * 128, :], rd=rt("xdram%d" % l, [t]), wr=[r_xt[xi]])
            c = l * NT + t
            act(hb[:], xt[xi][:], AF.Square, rd=[r_xt[xi], res("ssq")], wr=[res("hb"), res("ssq%d" % c)],
                accum_out=ssq[:, c:c + 1])
            act(rstd[:, c:c + 1], ssq[:, c:c + 1], AF.Sqrt, rd=[res("ssq%d" % c)], wr=[res("rstd%d" % c)],
                scale=1.0 / D, bias=EPS)
            P.op("dve", lambda e, c=c: e.reciprocal(out=rstd[:, c:c + 1], in_=rstd[:, c:c + 1]),
                 rd=[res("rstd%d" % c)], wr=[res("rstd%d" % c)])
            ts("dve", hb[:], xt[xi][:], rstd[:, c:c + 1], None, ALU.mult, rd=[r_xt[xi], res("rstd%d" % c)], wr=[res("hb")])

            def sfn(k, cond=cond):
                return (g1c[:, k * 2 + cond:k * 2 + cond + 1], modc[:, k * 2 + cond:k * 2 + cond + 1],
                        [res("g1c"), res("modc")])
            transposes_to(lambda k, t=t: hT[:, k, t * 128:(t + 1) * 128], hb, 8, rd=[res("hb")], wr=rt("hT", [t]),
                          scale_fn=sfn)

        def tokmajor_block(c0, ncols, consume):
            s = nxt("w", 3)
            wload(wbuf[s][:, :, 0:ncols], win_view(l, c0, ncols), s)
            for t in range(NT):
                b = nxt("ps", 4)
                for kc in range(8):
                    mm(ps[b][:, 0:ncols], hT[:, kc, t * 128:(t + 1) * 128], wbuf[s][:, kc, 0:ncols], kc == 0, kc == 7,
                       rd=rt("hT", [t]) + [r_wbuf[s]], wr=[r_ps[b]])
                consume(t, b)

        def featmajor_block(c0, nchunk, consume):
            s = nxt("w", 3)
            wload(wbuf[s][:, :, 0:nchunk * 128], win_view(l, c0, nchunk * 128), s)
            for m in range(nchunk):
                for tb in range(3):
                    b = nxt("ps", 4)
                    for kc in range(8):
                        mm(ps[b][:], wbuf[s][:, kc, m * 128:(m + 1) * 128], hT[:, kc, tb * 512:(tb + 1) * 512], kc == 0, kc == 7,
                           rd=rt("hT", range(tb * 4, tb * 4 + 4)) + [r_wbuf[s]], wr=[r_ps[b]])
                    consume(m, tb, b)

        def qk_norm(t, b, is_k):
            f = nxt("F", 4)
            cp("act", Ft[f][:], ps[b][:], rd=[r_ps[b]], wr=[r_Ft[f]])
            tt("dve", sq[:], Ft[f][:], Ft[f][:], ALU.mult, rd=[r_Ft[f]], wr=[res("sq")])
            si = nxt("st8", 2)
            P.op("dve", lambda e: e.tensor_reduce(out=st8[si][:], in_=sq[:].rearrange("p (h d) -> p h d", h=8),
                                                  axis=AX.X, op=ALU.add), rd=[res("sq")], wr=[res("st8_%d" % si)])
            act(st8[si][:], st8[si][:], AF.Sqrt, rd=[res("st8_%d" % si)], wr=[res("st8_%d" % si)], scale=1.0 / 64, bias=EPS)
            P.op("dve", lambda e: e.reciprocal(out=st8[si][:], in_=st8[si][:]), rd=[res("st8_%d" % si)], wr=[res("st8_%d" % si)])
            bi = nxt("B", 4)
            f3 = Ft[f][:].rearrange("p (h d) -> p h d", h=8)
            if not is_k:
                tt("dve", Bt[bi][:].rearrange("p (h d) -> p h d", h=8), f3, st8[si][:].unsqueeze(2).to_broadcast([128, 8, 64]),
                   ALU.mult, rd=[r_Ft[f], res("st8_%d" % si)], wr=[r_Bt[bi]])
                transposes_to(lambda j: QT2[:, j, t * 128:(t + 1) * 128], Bt[bi], 4, rd=[r_Bt[bi]], wr=rt("QT2", [t]),
                              scale_ap=gq8[:, 0:1])
            else:
                tt("dve", f3, f3, st8[si][:].unsqueeze(2).to_broadcast([128, 8, 64]), ALU.mult,
                   rd=[r_Ft[f], res("st8_%d" % si)], wr=[r_Ft[f]])
                tt("pool", f3, f3, gkb[:].unsqueeze(1).to_broadcast([128, 8, 64]), ALU.mult,
                   rd=[r_Ft[f], res("gkb")], wr=[r_Ft[f]])
                if t < 4:
                    sq_, half = t // 2, t % 2
                    dst = newk[sq_, l].rearrange("h t d -> t h d")[half * 128:(half + 1) * 128]
                    dma("sp", dst, f3, rd=[r_Ft[f]])
                cp("pool", Bt[bi][:], Ft[f][:], rd=[r_Ft[f]], wr=[r_Bt[bi]])
                if t >= 10:
                    r0 = 1024 + (t - 10) * 128
                    cc_dma(l, cc_in[l][r0:r0 + 128, :], Bt[bi][:], [r_Bt[bi]])
                transposes_to(lambda j: KT2[:, j, t * 128:(t + 1) * 128], Bt[bi], 4, rd=[r_Bt[bi]], wr=rt("KT2", [t]))

        tokmajor_block(0, 512, lambda t, b: qk_norm(t, b, False))
        tokmajor_block(512, 512, lambda t, b: qk_norm(t, b, True))

        def v_consume(t, b):
            if t < 4:
                f = nxt("F", 4)
                cp("act", Ft[f][:], ps[b][:], rd=[r_ps[b]], wr=[r_Ft[f]])
                sq_, half = t // 2, t % 2
                dst = newv[sq_, l].rearrange("h t d -> t h d")[half * 128:(half + 1) * 128]
                dma("sp", dst, Ft[f][:].rearrange("p (h d) -> p h d", h=8), rd=[r_Ft[f]])
                cp("dve", V[:, t, :], Ft[f][:], rd=[r_Ft[f]], wr=rt("V", [t]))
            else:
                evac(V[:, t, :], ps[b][:], rd=[r_ps[b]], wr=rt("V", [t]))
            if t >= 10:
                r0 = 1280 + (t - 10) * 128
                cc_dma(l, cc_in[l][r0:r0 + 128, :], V[:, t, :], rt("V", [t]))
        tokmajor_block(1024, 512, v_consume)

        def za_consume(m, tb, b):
            act(zaT[:, m, tb * 512:(tb + 1) * 512], ps[b][:], AF.Silu, rd=[r_ps[b]], wr=rt("zaT", range(tb * 4, tb * 4 + 4)))
        featmajor_block(1536, 4, za_consume)

        def ubzb_consume(m, tb, b):
            if m < 2:
                evac(ubT[:, m, tb * 512:(tb + 1) * 512], ps[b][:], rd=[r_ps[b]], wr=rt("ubT", range(tb * 4, tb * 4 + 4)))
            else:
                act(zbT[:, m - 2, tb * 512:(tb + 1) * 512], ps[b][:], AF.Silu, rd=[r_ps[b]], wr=rt("zbT", range(tb * 4, tb * 4 + 4)))
        featmajor_block(2048, 4, ubzb_consume)

        def uc_consume(t, b):
            evac(UC[:, t, :], ps[b][:, 0:256], rd=[r_ps[b]], wr=rt("UC", [t]))
            if t == 11:
                cc_dma(l, cc_in[l][1536:1664, 0:256], UC[:, t, :], rt("UC", [t]))
        tokmajor_block(2560, 256, uc_consume)

        def zc_consume(m, tb, b):
            act(zcT[:, m, tb * 512:(tb + 1) * 512], ps[b][:], AF.Silu, rd=[r_ps[b]], wr=rt("zcT", range(tb * 4, tb * 4 + 4)))
        featmajor_block(2816, 2, zc_consume)

        def ucus_tile(t, dst_ap, wr):
            b = nxt("ps", 4)
            for cb in range(2):
                mm(ps[b][:, cb * 256:(cb + 1) * 256], ubT[:, cb, t * 128:(t + 1) * 128], cs[:], True, True,
                   rd=rt("ubT", [t]) + [res("cs")], wr=[r_ps[b]])
            evac(dst_ap.rearrange("p (s c h) -> p c s h", s=2, c=2), ps[b][:].rearrange("p (c s h) -> p c s h", c=2, s=2),
                 rd=[r_ps[b]], wr=wr)

        for t in range(4, NT):
            bi = nxt("B", 4)
            ucus_tile(t, Bt[bi][:], [r_Bt[bi]])
            cc_dma(l, cc_in[l][(t - 4) * 128:(t - 3) * 128, :], Bt[bi][:], [r_Bt[bi]])

        groups = [[0, 1], [2, 3], [4, 5], [6, 7]]
        P.op("pool", lambda e, l=l: e.collective_compute("AllGather", ALU.bypass, replica_groups=groups,
                                                         ins=[cc_in[l]], outs=[cc_out[l]]),
             rd=list(ccin_all[l]), wr=[res("ccout%d" % l)], kind="cc", inc=1)

        for tl in range(2):
            dma("pool", Kctx[:, tl, :].rearrange("p (h d) -> p h d", h=8),
                cache_k[l][:, tl * 128:(tl + 1) * 128, :].rearrange("h p d -> p h d"), wr=[res("Kctx%d" % tl)])
            dma("pool", Vctx[:, tl, :].rearrange("p (h d) -> p h d", h=8),
                cache_v[l][:, tl * 128:(tl + 1) * 128, :].rearrange("h p d -> p h d"), wr=[res("Vctx%d" % tl)])
        for tl in range(2):
            transposes_to(lambda j, tl=tl: KTc2[:, j, tl * 128:(tl + 1) * 128], Kctx[:, tl, :], 4, rd=[res("Kctx%d" % tl)], wr=[res("KTc2")])

        def attend(j, par, q0, nq, keytiles, zres):
            h0, h1 = par * 64, par * 64 + 64
            r_o, r_m = r_ps[4], r_ps[5]
            flat = []
            for (kt, v, rd, chunks) in keytiles:
                for ch in chunks:
                    flat.append((kt, v, rd, ch))
            pend = None
            covered = False
            for i, (kt, v, rd, (c0, n, bias, brd)) in enumerate(flat):
                b = nxt("ps", 4)
                mm(ps[b][:, 0:n], kt, QT2[h0:h1, j, q0 + c0:q0 + c0 + n], True, bias is None,
                   rd=list(rd) + zres["q"], wr=[r_ps[b]])
                if bias is not None:
                    mm(ps[b][:, 0:n], ident[:], bias, False, True, rd=[res("ident")] + brd, wr=[r_ps[b]])
                cur = (b, v, rd, c0, n, i == 0)
                if pend is not None:
                    _pv(pend, r_o, r_m)
                pend = cur
            _pv(pend, r_o, r_m)
            f1 = nxt("F", 4)
            P.op("dve", lambda e: e.reciprocal(out=Ft[f1][h0:h1, 0:nq], in_=ps[5][h0:h1, 0:nq]), rd=[r_m], wr=[r_Ft[f1]])
            f2 = nxt("F", 4)
            tt("dve", Ft[f2][h0:h1, 0:nq], ps[4][h0:h1, 0:nq], Ft[f1][h0:h1, 0:nq], ALU.mult, rd=[r_o, r_Ft[f1]], wr=[r_Ft[f2]])
            tt("pool", zaT[h0:h1, j, q0:q0 + nq], Ft[f2][h0:h1, 0:nq], zaT[h0:h1, j, q0:q0 + nq], ALU.mult,
               rd=[r_Ft[f2]] + zres["z"], wr=zres["z"])

        def _pv(pd, r_o, r_m):
            b, v, rd, c0, n, first = pd
            bi = nxt("B", 4)
            act(Bt[bi][:, 0:n], ps[b][:, 0:n], AF.Exp, rd=[r_ps[b]], wr=[r_Bt[bi]])
            mm(ps[4][:, c0:c0 + n], v, Bt[bi][:, 0:n], first, False, rd=list(rd) + [r_Bt[bi]], wr=[r_o])
            mm(ps[5][:, c0:c0 + n], onesb[:], Bt[bi][:, 0:n], first, False, rd=[res("onesb"), r_Bt[bi]], wr=[r_m])

        for sq_ in range(2):
            tl = [2 * sq_, 2 * sq_ + 1]
            for h in range(8):
                j, par = h // 2, h % 2
                kts = []
                for t in tl:
                    kts.append((KT2[par * 64:par * 64 + 64, j, t * 128:(t + 1) * 128], V[:, t, j * 128:(j + 1) * 128],
                                rt("KT2", [t]) + rt("V", [t]), [(0, 256, None, [])]))
                attend(j, par, sq_ * 256, 256, kts, dict(q=rt("QT2", tl), z=rt("zaT", tl)))

        for rk in range(2):
            base = rk * 1664
            dma("sp", Khalo[:, rk * 2:rk * 2 + 2, :], cc_out[l][base + 1024:base + 1280, :].rearrange("(t p) n -> p t n", p=128),
                rd=[res("ccout%d" % l)], wr=[res("Khalo")])
            dma("sp", Vh[:, rk * 2:rk * 2 + 2, :], cc_out[l][base + 1280:base + 1536, :].rearrange("(t p) n -> p t n", p=128),
                rd=[res("ccout%d" % l)], wr=[res("Vh")])
            dma("sp", ucE[:, rk, :], cc_out[l][base + 1536:base + 1664, 0:256], rd=[res("ccout%d" % l)], wr=[res("ucE")])
        for hi in range(4):
            transposes_to(lambda j, hi=hi: KTh2[:, j, hi * 128:(hi + 1) * 128], Khalo[:, hi, :], 4, rd=[res("Khalo")], wr=[res("KTh2")])

        for h in range(8):
            j, par = h // 2, h % 2
            for qh in range(2):
                tb = nxt("tt", 1)
                dma("sp", ttb[tb][:], tt_d[l, h, qh], wr=[res("ttb%d" % tb)])
                kts = []
                for tl in range(2):
                    kts.append((KTc2[par * 64:par * 64 + 64, j, tl * 128:(tl + 1) * 128], Vctx[:, tl, j * 128:(j + 1) * 128],
                                [res("KTc2"), res("Vctx%d" % tl)], [(0, 512, None, [])]))
                for (kind, idx, row0, nrows, off) in NA_SCHED[qh]:
                    c0 = (row0 - qh * 8) * 64
                    n = nrows * 64
                    bias = ttb[tb][:, off * 64:off * 64 + n]
                    if kind == "own":
                        t = 4 + idx
                        kts.append((KT2[par * 64:par * 64 + 64, j, t * 128:(t + 1) * 128], V[:, t, j * 128:(j + 1) * 128],
                                    rt("KT2", [t]) + rt("V", [t]), [(c0, n, bias, [res("ttb%d" % tb)])]))
                    else:
                        kts.append((KTh2[par * 64:par * 64 + 64, j, idx * 128:(idx + 1) * 128], Vh[:, idx, j * 128:(j + 1) * 128],
                                    [res("KTh2"), res("Vh")], [(c0, n, bias, [res("ttb%d" % tb)])]))
                tiles = list(range(4 + qh * 4, 8 + qh * 4))
                attend(j, par, 512 + qh * 512, 512, kts, dict(q=rt("QT2", tiles), z=rt("zaT", tiles)))

        allA2 = rt("QT2", range(NT)) + rt("KT2", range(NT))
        for t in range(4):
            ucus_tile(t, UCUS[:, t, :], allA2 + [res("UCUS")])
        for rk in range(2):
            base = rk * 1664
            dma("sp", UCUS[:, 4 + rk * 8:12 + rk * 8, :], cc_out[l][base:base + 1024, :].rearrange("(t p) n -> p t n", p=128),
                rd=[res("ccout%d" % l)], wr=allA2 + [res("UCUS")])

        def fourier_group(q0, nq, src_tiles, table_fn, zres):
            banks = [nxt("ps", 4) for _ in range(2)]
            nsrc = len(src_tiles)
            for i, g in enumerate(src_tiles):
                tab, trd = table_fn(i)
                for cs_i in range(2):
                    for cb in range(2):
                        mm(ps[banks[cb]][:, 0:nq], UCUS[:, g, cs_i * 256 + cb * 128:cs_i * 256 + (cb + 1) * 128],
                           tab[:, cs_i, :], (i == 0 and cs_i == 0), (i == nsrc - 1 and cs_i == 1),
                           rd=[res("UCUS")] + trd, wr=[r_ps[banks[cb]]])
            fb = []
            for cb in range(2):
                bi = nxt("B", 4)
                evac(Bt[bi][:, 0:nq], ps[banks[cb]][:, 0:nq], rd=[r_ps[banks[cb]]], wr=[r_Bt[bi]])
                fb.append(bi)
            for m in range(2):
                b = nxt("ps", 4)
                for cb in range(2):
                    mm(ps[b][:, 0:nq], wfn[:, cb, m * 128:(m + 1) * 128], Bt[fb[cb]][:, 0:nq], cb == 0, cb == 1,
                       rd=[res("wfn"), r_Bt[fb[cb]]], wr=[r_ps[b]])
                tt("dve", zbT[:, m, q0:q0 + nq], ps[b][:, 0:nq], zbT[:, m, q0:q0 + nq], ALU.mult, rd=[r_ps[b]] + zres, wr=zres)

        for sq_ in range(2):
            fourier_group(sq_ * 256, 256, [2 * sq_, 2 * sq_ + 1],
                          lambda i: (dftp[:, i, :, :], [res("dftp")]), rt("zbT", [2 * sq_, 2 * sq_ + 1]))
        for qh in range(2):
            slabs = {}

            def tab_fn(i, qh=qh):
                di = nxt("ds", 2)
                dma("sp", dslab[di][:, :, 0:512], dfts_d[i, :, :, qh * 512:(qh + 1) * 512], wr=[res("dslab%d" % di)])
                return dslab[di][:, :, 0:512], [res("dslab%d" % di)]
            fourier_group(512 + qh * 512, 512, list(range(4, 20)), tab_fn, rt("zbT", range(4 + qh * 4, 8 + qh * 4)))

        for t in range(NT):
            dma("sp", bndb[:], bnd_d[t], wr=[res("bndb")])
            b = nxt("ps", 4)
            slots = _pool_slots(t)
            for g in range(4):
                for k, (kind, idx) in enumerate(slots):
                    if kind == "own":
                        src, srd = UC[:, idx, g * 64:(g + 1) * 64], rt("UC", [idx])
                    else:
                        src, srd = ucE[:, idx, g * 64:(g + 1) * 64], [res("ucE")]
                    mm(ps[b][:, g * 64:(g + 1) * 64], bndb[:, g, k, :], src, k == 0, k == len(slots) - 1,
                       rd=[res("bndb")] + srd, wr=[r_ps[b]])
            bi = nxt("B", 4)
            evac(Bt[bi][:, 0:256], ps[b][:, 0:256], rd=[r_ps[b]], wr=[r_Bt[bi]])
            b2 = nxt("B", 4)
            transposes_to(lambda c, b2=b2: Bt[b2][:, c * 128:(c + 1) * 128], Bt[bi], 2, rd=[r_Bt[bi]], wr=[r_Bt[b2]])
            for m in range(2):
                b3 = nxt("ps", 4)
                mm(ps[b3][:, 0:128], wpl[:, m, :], Bt[b2][:, m * 128:(m + 1) * 128], True, True,
                   rd=[res("wpl"), r_Bt[b2]], wr=[r_ps[b3]])
                f = nxt("F", 4)
                ts("dve", Ft[f][:, 0:128], ps[b3][:, 0:128], pscT[:, m:m + 1], None, ALU.mult, rd=[r_ps[b3], res("pscT")], wr=[r_Ft[f]])
                tt("pool", zcT[:, m, t * 128:(t + 1) * 128], Ft[f][:, 0:128], zcT[:, m, t * 128:(t + 1) * 128], ALU.mult,
                   rd=[r_Ft[f]] + rt("zcT", [t]), wr=rt("zcT", [t]))

        GA = [(zaT, 4, "zaT", p_a), (zbT, 2, "zbT", p_b), (zcT, 2, "zcT", p_c)]
        koff = [0, 4, 6]
        for nb in range(2):
            for br in range(3):
                gbuf, nk, gname, pw = GA[br]
                dma("pool", pbuf[:, koff[br]:koff[br] + nk, :],
                    pw[l].rearrange("(kc p) n -> p kc n", p=128)[:, :, nb * 512:(nb + 1) * 512],
                    wr=[res("pbuf%d" % br)])
            for br in range(3):
                gbuf, nk, gname, pw = GA[br]
                s = nxt("w", 3)
                wload(wbuf[s][:, :, :], win_view(l, 3072 + br * 1024 + nb * 512, 512), s)
                for t in range(NT):
                    b = nxt("ps", 4)
                    for kc in range(8):
                        mm(ps[b][:], hT[:, kc, t * 128:(t + 1) * 128], wbuf[s][:, kc, :], kc == 0, kc == 7,
                           rd=rt("hT", [t]) + [r_wbuf[s]], wr=[r_ps[b]])
                    f = nxt("F", 4)
                    act(Ft[f][:], ps[b][:], AF.Sigmoid, rd=[r_ps[b]], wr=[r_Ft[f]])
                    b2 = nxt("ps", 4)
                    for kc in range(nk):
                        mm(ps[b2][:], gbuf[:, kc, t * 128:(t + 1) * 128], pbuf[:, koff[br] + kc, :], kc == 0, kc == nk - 1,
                           rd=rt(gname, [t]) + [res("pbuf%d" % br)], wr=[r_ps[b2]])
                    mdst = MRG[:, t, nb * 512:(nb + 1) * 512]
                    mres = [res("MRG_%d_%d" % (t, nb))] + (allA2 + [res("UCUS")] if br == 0 else [])
                    if br == 0:
                        tt("dve", mdst, ps[b2][:], Ft[f][:], ALU.mult, rd=[r_ps[b2], r_Ft[f]], wr=mres)
                    else:
                        tt("dve", Ft[f][:], ps[b2][:], Ft[f][:], ALU.mult, rd=[r_ps[b2], r_Ft[f]], wr=[r_Ft[f]])
                        tt("pool", mdst, mdst, Ft[f][:], ALU.add, rd=[r_Ft[f]] + mres, wr=mres)

        allA3 = rt("zaT", range(NT)) + rt("zbT", range(NT)) + rt("zcT", range(NT))
        for t in range(NT):
            for half in range(2):
                b = nxt("pst", 2)
                for c in range(4):
                    tr(pst[b][:, c * 128:(c + 1) * 128], MRG[:, t, half * 512 + c * 128:half * 512 + (c + 1) * 128],
                       rd=[res("MRG_%d_%d" % (t, half))] + allA2 + [res("UCUS")], wr=[r_pst[b]])
                evac(MT[:, half * 4:half * 4 + 4, t * 128:(t + 1) * 128],
                     pst[b][:, 0:512].rearrange("p (c n) -> p c n", c=4), rd=[r_pst[b]],
                     wr=rt("zaT", [t]) + rt("zbT", [t]) + rt("zcT", [t]))
        for nb in range(2):
            s = nxt("w", 3)
            wload(wbuf[s][:, :, :], w_o[l].rearrange("(kc p) n -> p kc n", p=128)[:, :, nb * 512:(nb + 1) * 512], s)
            for t in range(NT):
                cond = 0 if t < 4 else 1
                b = nxt("ps", 4)
                for kc in range(8):
                    mm(ps[b][:], MT[:, kc, t * 128:(t + 1) * 128], wbuf[s][:, kc, :], kc == 0, kc == 7,
                       rd=rt("zaT", [t]) + rt("zbT", [t]) + rt("zcT", [t]) + [r_wbuf[s]], wr=[r_ps[b]])
                f = nxt("F", 4)
                dma("sp", Ft[f][:], xsrc[t * 128:(t + 1) * 128, nb * 512:(nb + 1) * 512], rd=rt("xdram%d" % l, [t]), wr=[r_Ft[f]])
                f2 = nxt("F", 4)
                tt("dve", Ft[f2][:], ps[b][:], gate_bc[cond][:, nb * 512:(nb + 1) * 512], ALU.mult,
                   rd=[r_ps[b], res("gate_bc%d" % cond)], wr=[r_Ft[f2]])
                tt("pool", Ft[f2][:], Ft[f2][:], Ft[f][:], ALU.add, rd=[r_Ft[f2], r_Ft[f]], wr=[r_Ft[f2]])
                dma("sp", xdst[t * 128:(t + 1) * 128, nb * 512:(nb + 1) * 512], Ft[f2][:], rd=[r_Ft[f2]],
                    wr=rt("xdram%d" % (l + 1), [t]))

    P.emit(nc)
    st.close()
    return nc


def _tok_global(rank, j):
    j = np.asarray(j)
    return j if rank == 0 else 2047 - j


def _bias_tables(rpb, rank):
    out = np.full((2, 8, 2, 128, NBLK * 64), NEG, np.float32)
    for qh in range(2):
        for (kind, idx, row0, nrows, off) in NA_SCHED[qh]:
            if kind == "own":
                kloc = (2 * idx) * 64 + np.arange(128)
                kg = _tok_global(rank, kloc)
                valid_tile = True
            else:
                q_rank, ab = idx // 2, idx % 2
                kloc = 768 + ab * 128 + np.arange(128)
                kg = _tok_global(q_rank, kloc)
                valid_tile = (q_rank != rank)
            for r in range(nrows):
                p = row0 + r
                qg = _tok_global(rank, p * 64 + np.arange(64))
                qr, qc = qg // 64, qg % 64
                kr, kc = kg // 64, kg % 64
                rs = np.clip(qr - 4, 0, 24)
                cst = np.clip(qc - 8, 0, 48)
                ok = ((kr[:, None] >= rs[None, :]) & (kr[:, None] < rs[None, :] + 8) &
                      (kc[:, None] >= cst[None, :]) & (kc[:, None] < cst[None, :] + 16))
                if not valid_tile:
                    ok[:] = False
                ri = np.clip(kr[:, None] - qr[None, :] + 7, 0, 14)
                ci = np.clip(kc[:, None] - qc[None, :] + 15, 0, 30)
                vals = rpb[:, :, ri, ci]
                blk = np.where(ok[None, None], vals, NEG)
                c0 = (off + r) * 64
                out[:, :, qh, :, c0:c0 + 64] = blk
    return out.astype(NPBF)


def _dft_tables(rank):
    g = np.arange(2048)
    gin = np.where(g < 1024, g, 2047 - (g - 1024)).astype(np.float64)
    tout = _tok_global(rank, np.arange(1024)).astype(np.float64)
    ang = 2 * np.pi * np.outer(gin, tout) / 2048.0
    nrm = 1.0 / np.sqrt(2048.0 * 64.0)
    tab = np.stack([np.cos(ang) * nrm, -np.sin(ang) * nrm], axis=1)
    return tab.reshape(16, 128, 2, 1024).astype(NPBF)


def _dft_prompt():
    t = np.arange(256, dtype=np.float64)
    ang = 2 * np.pi * np.outer(t, t) / 256.0
    nrm = 1.0 / np.sqrt(256.0 * 64.0)
    tab = np.stack([np.cos(ang) * nrm, -np.sin(ang) * nrm], axis=1)
    tab = tab.reshape(2, 128, 2, 256).transpose(1, 0, 2, 3)
    return np.ascontiguousarray(tab).astype(NPBF)


def _cs_table():
    c = np.arange(128)
    same = (c[:, None] // 64) == (c[None, :] // 64)
    ang = 2 * np.pi * np.outer(c % 64, c % 64) / 64.0
    cc = np.where(same, np.cos(ang), 0.0)
    sc = np.where(same, np.sin(ang), 0.0)
    return np.concatenate([cc, sc], axis=1).astype(NPBF)


def _pool_w(spos, tpos, L, w, same):
    lo = np.clip(tpos - w // 2, 0, L)
    hi = np.clip(tpos + w // 2, 0, L)
    cnt = (hi - lo).astype(np.float64)
    inw = (spos[:, None] >= lo[None, :]) & (spos[:, None] < hi[None, :])
    m = np.where(inw, 1.0 / cnt[None, :], 0.0) - (spos[:, None] == tpos[None, :]).astype(np.float64)
    return m if same else m * 0.0


def _band_tables(rank):
    out = np.zeros((NT, 128, 4, 4, 128), np.float32)
    wins = (2, 4, 8, 16)
    for i in range(NT):
        for k, (kind, idx) in enumerate(_pool_slots(i)):
            if i < 4:
                tpos = (i % 2) * 128 + np.arange(128)
                spos = (idx % 2) * 128 + np.arange(128)
                L, ok = 256, True
            else:
                tpos = _tok_global(rank, (i - 4) * 128 + np.arange(128))
                L = 2048
                if kind == "own":
                    spos = _tok_global(rank, (idx - 4) * 128 + np.arange(128))
                    ok = True
                else:
                    spos = _tok_global(idx, 896 + np.arange(128))
                    ok = (idx != rank)
            for g, w in enumerate(wins):
                out[i, :, g, k, :] = _pool_w(spos, tpos, L, w, ok)
    return out.astype(NPBF)


_NC_CACHE = {}


def kernel(x_prompt, x_sample, cache_k, cache_v, c, c_ctx, norm_g, w_ada, b_ada, w_in,
           q_norm_g, k_norm_g, rpb, w_fnet, w_pool, pool_scale, p_a, p_b, p_c, w_o):
    f32 = lambda a: np.ascontiguousarray(np.asarray(a, dtype=np.float32))
    x_prompt, x_sample, cache_k, cache_v = f32(x_prompt), f32(x_sample), f32(cache_k), f32(cache_v)
    c, c_ctx, norm_g, w_ada, b_ada, w_in = f32(c), f32(c_ctx), f32(norm_g), f32(w_ada), f32(b_ada), f32(w_in)
    q_norm_g, k_norm_g, rpb, w_fnet, w_pool = f32(q_norm_g), f32(k_norm_g), f32(rpb), f32(w_fnet), f32(w_pool)
    pool_scale, p_a, p_b, p_c, w_o = f32(pool_scale), f32(p_a), f32(p_b), f32(p_c), f32(w_o)

    if "nc" not in _NC_CACHE:
        _NC_CACHE["nc"] = build_nc()
    nc = _NC_CACHE["nc"]

    b_adaT = np.ascontiguousarray(b_ada.reshape(2, 24, 128).transpose(0, 2, 1))
    norm_gT = np.ascontiguousarray(norm_g.reshape(2, 8, 128).transpose(0, 2, 1))
    gqT = np.ascontiguousarray(np.concatenate([q_norm_g, q_norm_g], axis=1)[:, :, None])
    w_poolbd = np.zeros((2, 2, 128, 128), np.float32)
    for l in range(2):
        for g in range(4):
            cb, o = g // 2, (g % 2) * 64
            w_poolbd[l, cb, o:o + 64, o:o + 64] = w_pool[l, g]
    pool_scaleT = np.ascontiguousarray(pool_scale.reshape(2, 2, 128).transpose(0, 2, 1))
    ident = np.eye(128, dtype=np.float32)
    tabs = {}
    for rank in range(2):
        tabs[rank] = dict(tt=_bias_tables(rpb, rank), dfts=_dft_tables(rank), bnd=_band_tables(rank))
    dftp = _dft_prompt()
    cs = _cs_table()

    in_maps = []
    for core in range(8):
        b, rank = core // 2, core % 2
        xs = x_sample[b, 0:1024] if rank == 0 else x_sample[b, 1024:2048][::-1]
        xin = np.ascontiguousarray(np.concatenate([x_prompt[2 * core], x_prompt[2 * core + 1], xs], axis=0))
        cond = np.stack([c_ctx, c[b]], axis=0)
        condT = np.ascontiguousarray(cond.reshape(2, 8, 128).transpose(2, 1, 0).reshape(128, 16))
        in_maps.append(dict(
            xin=xin, condT=condT, w_ada=w_ada, b_adaT=b_adaT, norm_gT=norm_gT, w_in=w_in, gqT=gqT,
            k_norm_g=k_norm_g, cache_k=np.ascontiguousarray(cache_k[b]), cache_v=np.ascontiguousarray(cache_v[b]),
            w_fnet=w_fnet, w_poolbd=w_poolbd, pool_scaleT=pool_scaleT, p_a=p_a, p_b=p_b, p_c=p_c, w_o=w_o,
            ident=ident.astype(NPBF), identf=ident, tt=tabs[rank]["tt"], dfts=tabs[rank]["dfts"], dftp=dftp,
            cs=cs, bnd=tabs[rank]["bnd"]))

    res = run_bass_kernel_spmd(nc, in_maps, core_ids=list(range(8)))
    y_prompt = np.zeros((16, 256, D), np.float32)
    y_sample = np.zeros((4, 2048, D), np.float32)
    new_k = np.zeros((16, 2, 8, 256, 64), np.float32)
    new_v = np.zeros((16, 2, 8, 256, 64), np.float32)
    for core in range(8):
        r = res.results[core]
        b, rank = core // 2, core % 2
        yo = np.asarray(r["yout"], dtype=np.float32)
        y_prompt[2 * core] = yo[0:256]
        y_prompt[2 * core + 1] = yo[256:512]
        if rank == 0:
            y_sample[b, 0:1024] = yo[512:1536]
        else:
            y_sample[b, 1024:2048] = yo[512:1536][::-1]
        nk = np.asarray(r["newk"], dtype=np.float32)
        nv = np.asarray(r["newv"], dtype=np.float32)
        new_k[2 * core], new_k[2 * core + 1] = nk[0], nk[1]
        new_v[2 * core], new_v[2 * core + 1] = nv[0], nv[1]
    return (y_prompt, y_sample, new_k, new_v)
```

```python
import numpy as np
import ml_dtypes
from contextlib import ExitStack
import concourse.bass as bass
import concourse.mybir as mybir
from concourse.bass_utils import run_bass_kernel_spmd

F32 = mybir.dt.float32
BF16 = mybir.dt.bfloat16
ALU = mybir.AluOpType
AF = mybir.ActivationFunctionType
AX = mybir.AxisListType
NPBF = ml_dtypes.bfloat16

D = 1024
NT = 12
NTOK = NT * 128
EPS = 1e-6
NEG = -1e30
NBLK = 50


class Res:
    __slots__ = ("name", "w", "r")

    def __init__(self, name):
        self.name = name
        self.w = []
        self.r = []


class Prog:
    ENGS = ("pe", "act", "dve", "pool", "sp")
    NDMA = 8

    def __init__(self):
        self.ops = {e: [] for e in self.ENGS}
        self.dma_cnt = {e: 0 for e in self.ENGS}

    def op(self, eng, fn, rd=(), wr=(), kind="c", inc=None):
        ops = self.ops[eng]
        idx = len(ops)
        deps = []
        for r in rd:
            deps.extend(r.w)
        for r in wr:
            deps.extend(r.w)
            deps.extend(r.r)
        tok = (eng, idx, kind)
        rec = dict(fn=fn, deps=deps, kind=kind, signal=False, inc=inc)
        if kind != "c":
            rec["dq"] = self.dma_cnt[eng]
            self.dma_cnt[eng] += 1
        ops.append(rec)
        for r in wr:
            r.w = [tok]
            r.r = []
        for r in rd:
            if r not in wr:
                r.r.append(tok)
        return tok

    def emit(self, nc):
        ops = self.ops
        for e in self.ENGS:
            for rec in ops[e]:
                for (de, di, dk) in rec["deps"]:
                    if dk == "c" and not (de == "pe" and e == "pe"):
                        ops[de][di]["signal"] = True
        for e in self.ENGS:
            c = 0
            for rec in ops[e]:
                if rec["kind"] == "c" and rec["signal"]:
                    c += 1
                rec["sigval"] = c
        with ExitStack() as st:
            csem = {e: st.enter_context(nc.semaphore("c_" + e)) for e in self.ENGS}
            dsem = {e: [st.enter_context(nc.semaphore("d_%s%d" % (e, i))) for i in range(self.NDMA)]
                    for e in self.ENGS if self.dma_cnt[e] > 0}
            block = st.enter_context(nc.Block())
            for e in self.ENGS:
                per = [0] * self.NDMA
                for rec in ops[e]:
                    if rec["kind"] != "c":
                        s = rec["dq"] % self.NDMA
                        step = 16 if rec["inc"] is None else rec["inc"]
                        rec["dprev"] = per[s]
                        per[s] += step
                        rec["dval"] = per[s]
                        rec["dstep"] = step
                        rec["dsem"] = dsem[e][s]

            def run(e, eng):
                waited = {}
                for rec in ops[e]:
                    need = {}
                    for (de, di, dk) in rec["deps"]:
                        drec = ops[de][di]
                        if dk == "c":
                            if de == "pe" and e == "pe":
                                continue
                            key = ("c", de)
                            sem = csem[de]
                            val = drec["sigval"]
                        else:
                            sem = drec["dsem"]
                            key = ("d", id(sem))
                            val = drec["dval"]
                        if val > need.get(key, (None, 0))[1]:
                            need[key] = (sem, val)
                    if rec["kind"] != "c" and rec["dprev"] > 0:
                        sem = rec["dsem"]
                        key = ("d", id(sem))
                        if rec["dprev"] > need.get(key, (None, 0))[1]:
                            need[key] = (sem, rec["dprev"])
                    for key, (sem, val) in need.items():
                        if waited.get(key, 0) < val:
                            eng.wait_ge(sem, val)
                            waited[key] = val
                    ins = rec["fn"](eng)
                    if rec["kind"] == "c":
                        if rec["signal"]:
                            ins.then_inc(csem[e], 1)
                    else:
                        ins.then_inc(rec["dsem"], rec["dstep"])
                if e in dsem:
                    per = {}
                    for rec in ops[e]:
                        if rec["kind"] != "c":
                            per[id(rec["dsem"])] = (rec["dsem"], rec["dval"])
                    for sem, val in per.values():
                        eng.wait_ge(sem, val)

            if ops["pe"]:
                @block.tensor
                def _(eng):
                    run("pe", eng)
            if ops["act"]:
                @block.scalar
                def _(eng):
                    run("act", eng)
            if ops["dve"]:
                @block.vector
                def _(eng):
                    run("dve", eng)
            if ops["pool"]:
                @block.gpsimd
                def _(eng):
                    run("pool", eng)
            if ops["sp"]:
                @block.sync
                def _(eng):
                    run("sp", eng)


def _na_schedule():
    sched = []
    for qh in range(2):
        lo, hi = qh * 8, qh * 8 + 7
        items = []
        off = 0
        for j in range(8):
            p0 = 0 if j <= 3 else 2 * j - 4
            p1 = min(2 * j + 5, 15)
            a, b = max(p0, lo), min(p1, hi)
            if a <= b:
                items.append(("own", j, a, b - a + 1, off))
                off += b - a + 1
        if qh == 1:
            for hidx in range(4):
                items.append(("halo", hidx, 12, 4, off))
                off += 4
        assert off <= NBLK
        sched.append(items)
    return sched


NA_SCHED = _na_schedule()


def _pool_slots(i):
    if i < 4:
        s = 2 * (i // 2)
        return [("own", s), ("own", s + 1)]
    m = i - 4
    sl = []
    if m - 1 >= 0:
        sl.append(("own", 4 + m - 1))
    sl.append(("own", 4 + m))
    if m + 1 <= 7:
        sl.append(("own", 4 + m + 1))
    if m == 7:
        sl.append(("edge", 0))
        sl.append(("edge", 1))
    return sl


def build_nc():
    nc = bass.Bass("TRN2", target_bir_lowering=False)

    def din(name, shape, dt=F32):
        return nc.dram_tensor(name, list(shape), dt, kind="ExternalInput").ap()

    def dout(name, shape, dt=F32):
        return nc.dram_tensor(name, list(shape), dt, kind="ExternalOutput").ap()

    xin = din("xin", [NTOK, D])
    condT = din("condT", [128, 16])
    w_ada = din("w_ada", [2, D, 3 * D])
    b_adaT = din("b_adaT", [2, 128, 24])
    norm_gT = din("norm_gT", [2, 128, 8])
    w_in = din("w_in", [2, D, 6 * D])
    gqT = din("gqT", [2, 128, 1])
    k_norm_g = din("k_norm_g", [2, 64])
    cache_k = din("cache_k", [2, 8, 256, 64])
    cache_v = din("cache_v", [2, 8, 256, 64])
    w_fnet = din("w_fnet", [2, 256, 256])
    w_poolbd = din("w_poolbd", [2, 2, 128, 128])
    pool_scaleT = din("pool_scaleT", [2, 128, 2])
    p_a = din("p_a", [2, 512, D])
    p_b = din("p_b", [2, 256, D])
    p_c = din("p_c", [2, 256, D])
    w_o = din("w_o", [2, D, D])
    ident_d = din("ident", [128, 128], BF16)
    identf_d = din("identf", [128, 128])
    tt_d = din("tt", [2, 8, 2, 128, NBLK * 64], BF16)
    dfts_d = din("dfts", [16, 128, 2, 1024], BF16)
    dftp_d = din("dftp", [128, 2, 2, 256], BF16)
    cs_d = din("cs", [128, 256], BF16)
    bnd_d = din("bnd", [NT, 128, 4, 4, 128], BF16)

    yout = dout("yout", [NTOK, D])
    newk = dout("newk", [2, 2, 8, 256, 64])
    newv = dout("newv", [2, 2, 8, 256, 64])

    x1 = nc.dram_tensor("x1", [NTOK, D], F32).ap()
    cc_in = [nc.dram_tensor("cc_in%d" % l, [1664, 512], BF16).ap() for l in range(2)]
    cc_out = [nc.dram_tensor("cc_out%d" % l, [3328, 512], BF16).ap() for l in range(2)]

    P = Prog()
    st = ExitStack()

    def sb(name, shape, dt):
        return st.enter_context(nc.sbuf_tensor("s_" + name, list(shape), dt))

    hT = sb("hT", [128, 8, NTOK], BF16)
    A2 = sb("A2", [128, 12288], BF16)
    A3 = sb("A3", [128, 12288], BF16)
    A4 = sb("A4", [128, 12288], BF16)
    QT2 = A2[:, 0:6144].rearrange("p (j t) -> p j t", j=4)
    KT2 = A2[:, 6144:12288].rearrange("p (j t) -> p j t", j=4)
    UCUS = A2[:, 0:10240].rearrange("p (g n) -> p g n", g=20)
    MRG = A2[:, :].rearrange("p (t n) -> p t n", t=NT)
    zaT = A3[:, 0:6144].rearrange("p (j t) -> p j t", j=4)
    zbT = A3[:, 6144:9216].rearrange("p (j t) -> p j t", j=2)
    zcT = A3[:, 9216:12288].rearrange("p (j t) -> p j t", j=2)
    MT = A3[:, :].rearrange("p (k t) -> p k t", k=8)
    V = A4[:, 0:6144].rearrange("p (t n) -> p t n", t=NT)
    ubT = A4[:, 6144:9216].rearrange("p (j t) -> p j t", j=2)
    UC = A4[:, 9216:12288].rearrange("p (t n) -> p t n", t=NT)

    wbuf = [sb("wbuf%d" % i, [128, 8, 512], BF16) for i in range(3)]
    pbuf = sb("pbuf", [128, 8, 512], BF16)
    ttb = [sb("ttb%d" % i, [128, NBLK * 64], BF16) for i in range(1)]
    dslab = [sb("dslab%d" % i, [128, 2, 1024], BF16) for i in range(2)]
    dftp = sb("dftp", [128, 2, 2, 256], BF16)
    cs = sb("cs", [128, 256], BF16)
    ident = sb("identb", [128, 128], BF16)
    identf = sb("identfs", [128, 128], F32)
    onesb = sb("onesb", [128, 128], BF16)
    onesf = sb("onesf", [128, 128], F32)
    repf = [sb("repf%d" % i, [128, 128], F32) for i in range(2)]
    KTc2 = sb("KTc2", [128, 4, 256], BF16)
    Vctx = sb("Vctx", [128, 2, 512], BF16)
    Kctx = sb("Kctx", [128, 2, 512], BF16)
    KTh2 = sb("KTh2", [128, 4, 512], BF16)
    Khalo = sb("Khalo", [128, 4, 512], BF16)
    Vh = sb("Vh", [128, 4, 512], BF16)
    ucE = sb("ucE", [128, 2, 256], BF16)
    bndb = sb("bndb", [128, 4, 4, 128], BF16)
    gate_bc = [sb("gate_bc%d" % i, [128, D], F32) for i in range(2)]
    xt = [sb("xt%d" % i, [128, D], F32) for i in range(2)]
    hb = sb("hb", [128, D], BF16)
    Ft = [sb("Ft%d" % i, [128, 512], F32) for i in range(4)]
    Bt = [sb("Bt%d" % i, [128, 512], BF16) for i in range(4)]
    sq = sb("sq", [128, 512], F32)
    gkb = sb("gkb", [128, 64], F32)
    wfn = sb("wfn", [128, 2, 256], BF16)
    wpl = sb("wpl", [128, 2, 128], BF16)
    scol = sb("scol", [128, 16], F32)
    scb = sb("scb", [128, 16], BF16)
    modc = sb("modc", [128, 48], F32)
    g1c = sb("g1c", [128, 16], F32)
    badaT = sb("badaT", [128, 24], F32)
    ngT = sb("ngT", [128, 8], F32)
    gq8 = sb("gq8", [128, 1], F32)
    pscT = sb("pscT", [128, 2], F32)
    ssq = sb("ssq", [128, 2 * NT], F32)
    rstd = sb("rstd", [128, 2 * NT], F32)
    st8 = [sb("st8_%d" % i, [128, 8], F32) for i in range(2)]

    ps = [st.enter_context(nc.psum_tensor("ps%d" % i, [128, 512], F32)) for i in range(8)]
    pst = [ps[i][:].bitcast(BF16) for i in range(4)]

    R = {}

    def res(name):
        if name not in R:
            R[name] = Res(name)
        return R[name]

    r_ps = [res("ps%d" % i) for i in range(8)]
    r_pst = r_ps
    r_wbuf = [res("wbuf%d" % i) for i in range(3)]
    r_Ft = [res("Ft%d" % i) for i in range(4)]
    r_Bt = [res("Bt%d" % i) for i in range(4)]
    r_xt = [res("xt%d" % i) for i in range(2)]
    rot = {"ps": 0, "pst": 0, "w": 0, "F": 0, "B": 0, "xt": 0, "ev": 0, "st8": 0, "tt": 0, "ds": 0, "rep": 0, "om": 0}

    def nxt(kind, n):
        v = rot[kind]
        rot[kind] = (v + 1) % n
        return v

    def rt(buf, tiles):
        return [res("%s_%d" % (buf, t)) for t in tiles]

    def dma(q, out, in_, rd=(), wr=()):
        return P.op(q, lambda e: e.dma_start(out=out, in_=in_), rd=rd, wr=wr, kind="d")

    def mm(out, lhsT, rhs, start, stop, rd=(), wr=()):
        return P.op("pe", lambda e: e.matmul(out, lhsT=lhsT, rhs=rhs, start=start, stop=stop), rd=rd, wr=wr)

    def tr(out, in_, rd=(), wr=()):
        return P.op("pe", lambda e: e.transpose(out, in_, ident[:]), rd=list(rd) + [res("ident")], wr=wr)

    def act(out, in_, func, rd=(), wr=(), **kw):
        return P.op("act", lambda e: e.activation(out=out, in_=in_, func=func, **kw), rd=rd, wr=wr)

    def tt(eng, out, in0, in1, op, rd=(), wr=()):
        return P.op(eng, lambda e: e.tensor_tensor(out=out, in0=in0, in1=in1, op=op), rd=rd, wr=wr)

    def ts(eng, out, in0, s1, s2, op0, op1=None, rd=(), wr=()):
        if op1 is None:
            return P.op(eng, lambda e: e.tensor_scalar(out=out, in0=in0, scalar1=s1, scalar2=None, op0=op0), rd=rd, wr=wr)
        return P.op(eng, lambda e: e.tensor_scalar(out=out, in0=in0, scalar1=s1, scalar2=s2, op0=op0, op1=op1), rd=rd, wr=wr)

    def cp(eng, out, in_, rd=(), wr=()):
        if eng == "act":
            return act(out, in_, AF.Copy, rd=rd, wr=wr)
        return P.op(eng, lambda e: e.tensor_copy(out=out, in_=in_), rd=rd, wr=wr)

    def evac(out, in_, rd=(), wr=()):
        eng = "act" if nxt("ev", 2) == 0 else "dve"
        return cp(eng, out, in_, rd=rd, wr=wr)

    def memset(eng, ap, val, wr=()):
        return P.op(eng, lambda e: e.memset(ap, val), wr=wr)

    dma("sp", ident[:], ident_d, wr=[res("ident")])
    dma("sp", identf[:], identf_d, wr=[res("identf")])
    dma("sp", dftp[:], dftp_d, wr=[res("dftp")])
    dma("sp", cs[:], cs_d, wr=[res("cs")])
    dma("sp", scol[:], condT, wr=[res("scol")])
    memset("dve", onesb[:], 1.0, wr=[res("onesb")])
    memset("dve", onesf[:], 1.0, wr=[res("onesf")])
    act(scol[:], scol[:], AF.Silu, rd=[res("scol")], wr=[res("scol")])
    cp("dve", scb[:], scol[:], rd=[res("scol")], wr=[res("scb")])
    memset("dve", ssq[:], 0.0, wr=[res("ssq")])
    ccn = {"n": 0}

    def ccres(l):
        ccn["n"] += 1
        return res("ccin%d_%d" % (l, ccn["n"]))
    ccin_all = {0: [], 1: []}

    def cc_dma(l, dst, src, rd):
        r = ccres(l)
        ccin_all[l].append(r)
        dma("sp", dst, src, rd=rd, wr=[r])

    def wload(dst_ap, src_ap, slot):
        return dma("pool", dst_ap, src_ap, wr=[r_wbuf[slot]])

    def win_view(l, c0, n):
        return w_in[l].rearrange("(kc p) n -> p kc n", p=128)[:, :, c0:c0 + n]

    def transposes_to(dst_fn, src_tile, nchunks, rd, wr, scale_ap=None, bias_fn=None, scale_fn=None):
        b = nxt("ps", 4)
        for c in range(nchunks):
            tr(pst[b][:, c * 128:(c + 1) * 128], src_tile[:, c * 128:(c + 1) * 128], rd=rd, wr=[r_pst[b]])
        if scale_ap is None and scale_fn is None:
            for c in range(nchunks):
                evac(dst_fn(c), pst[b][:, c * 128:(c + 1) * 128], rd=[r_pst[b]], wr=wr)
        elif scale_fn is not None:
            for c in range(nchunks):
                act(dst_fn(c), pst[b][:, c * 128:(c + 1) * 128], AF.Identity, rd=[r_pst[b]] + scale_fn(c)[2], wr=wr,
                    scale=scale_fn(c)[0], bias=scale_fn(c)[1])
        else:
            for c in range(nchunks):
                act(dst_fn(c), pst[b][:, c * 128:(c + 1) * 128], AF.Identity, rd=[r_pst[b]] + [res("gq8")], wr=wr,
                    scale=scale_ap, bias=0.0)

    for l in range(2):
        xsrc = xin if l == 0 else x1
        xdst = x1 if l == 0 else yout
        L = "L%d" % l

        dma("sp", badaT[:], b_adaT[l], wr=[res("badaT")])
        dma("sp", ngT[:], norm_gT[l], wr=[res("ngT")])
        dma("sp", gq8[:], gqT[l], wr=[res("gq8")])
        dma("sp", gkb[:], k_norm_g[l:l + 1, :].partition_broadcast(128), wr=[res("gkb")])
        dma("sp", pscT[:], pool_scaleT[l], wr=[res("pscT")])
        dma("pool", wfn[:], w_fnet[l].rearrange("(kc p) n -> p kc n", p=128), wr=[res("wfn")])
        dma("pool", wpl[:], w_poolbd[l].rearrange("c p n -> p c n"), wr=[res("wpl")])
        act(gq8[:], gq8[:], AF.Copy, rd=[res("gq8")], wr=[res("gq8")], scale=0.125)
        for kc in range(8):
            s = nxt("w", 3)
            pm = nxt("ps", 4)
            wv = wbuf[s][:, :, :].rearrange("p a b -> p (a b)")[:, 0:3072]
            wload(wv, w_ada[l, kc * 128:(kc + 1) * 128, :], s)
            for n in range(24):
                mm(ps[pm][:, n * 2:(n + 1) * 2], wv[:, n * 128:(n + 1) * 128], scb[:, kc * 2:(kc + 1) * 2],
                   start=True, stop=True, rd=[r_wbuf[s], res("scb")], wr=[r_ps[pm]])
            if kc == 0:
                tt("dve", modc[:].rearrange("p (n c) -> p n c", c=2), ps[pm][:, 0:48].rearrange("p (n c) -> p n c", c=2),
                   badaT[:].unsqueeze(2).to_broadcast([128, 24, 2]), ALU.add, rd=[r_ps[pm], res("badaT")], wr=[res("modc")])
            else:
                tt("dve", modc[:], ps[pm][:, 0:48], modc[:], ALU.add, rd=[r_ps[pm], res("modc")], wr=[res("modc")])
        ts("dve", g1c[:], modc[:, 16:32], 1.0, None, ALU.add, rd=[res("modc")], wr=[res("g1c")])
        tt("dve", g1c[:].rearrange("p (n c) -> p n c", c=2), g1c[:].rearrange("p (n c) -> p n c", c=2),
           ngT[:].unsqueeze(2).to_broadcast([128, 8, 2]), ALU.mult, rd=[res("g1c"), res("ngT")], wr=[res("g1c")])
        for cond in range(2):
            for half in range(2):
                b = nxt("ps", 4)
                for c4 in range(4):
                    n = 16 + half * 4 + c4
                    rp = nxt("rep", 2)
                    ts("dve", repf[rp][:], onesf[:], modc[:, n * 2 + cond:n * 2 + cond + 1], None, ALU.mult,
                       rd=[res("onesf"), res("modc")], wr=[res("rep%d" % rp)])
                    mm(ps[b][:, c4 * 128:(c4 + 1) * 128], repf[rp][:], identf[:], True, True,
                       rd=[res("rep%d" % rp), res("identf")], wr=[r_ps[b]])
                evac(gate_bc[cond][:, half * 512:(half + 1) * 512], ps[b][:], rd=[r_ps[b]], wr=[res("gate_bc%d" % cond)])

        def stage0(t):
            cond = 0 if t < 4 else 1
            xi = nxt("xt", 2)
            dma("sp", xt[xi][:], xsrc[t * 128:(t + 1) * 128, :], rd=rt("xdram%d" % l, [t]), wr=[r_xt[xi]])
            c = l * NT + t
            act(hb[:], xt[xi][:], AF.Square, rd=[r_xt[xi], res("ssq")], wr=[res("hb"), res("ssq%d" % c)],
                accum_out=ssq[:, c:c + 1])
            act(rstd[:, c:c + 1], ssq[:, c:c + 1], AF.Sqrt, rd=[res("ssq%d" % c)], wr=[res("rstd%d" % c)],
                scale=1.0 / D, bias=EPS)
            P.op("dve", lambda e, c=c: e.reciprocal(out=rstd[:, c:c + 1], in_=rstd[:, c:c + 1]),
                 rd=[res("rstd%d" % c)], wr=[res("rstd%d" % c)])
            ts("dve", hb[:], xt[xi][:], rstd[:, c:c + 1], None, ALU.mult, rd=[r_xt[xi], res("rstd%d" % c)], wr=[res("hb")])

            def sfn(k, cond=cond):
                return (g1c[:, k * 2 + cond:k * 2 + cond + 1], modc[:, k * 2 + cond:k * 2 + cond + 1],
                        [res("g1c"), res("modc")])
            transposes_to(lambda k, t=t: hT[:, k, t * 128:(t + 1) * 128], hb, 8, rd=[res("hb")], wr=rt("hT", [t]),
                          scale_fn=sfn)

        def tokmajor_block(c0, ncols, consume):
            s = nxt("w", 3)
            wload(wbuf[s][:, :, 0:ncols], win_view(l, c0, ncols), s)
            for t in range(NT):
                b = nxt("ps", 4)
                for kc in range(8):
                    mm(ps[b][:, 0:ncols], hT[:, kc, t * 128:(t + 1) * 128], wbuf[s][:, kc, 0:ncols], kc == 0, kc == 7,
                       rd=rt("hT", [t]) + [r_wbuf[s]], wr=[r_ps[b]])
                consume(t, b)

        def featmajor_block(c0, nchunk, consume):
            s = nxt("w", 3)
            wload(wbuf[s][:, :, 0:nchunk * 128], win_view(l, c0, nchunk * 128), s)
            for m in range(nchunk):
                for tb in range(3):
                    b = nxt("ps", 4)
                    for kc in range(8):
                        mm(ps[b][:], wbuf[s][:, kc, m * 128:(m + 1) * 128], hT[:, kc, tb * 512:(tb + 1) * 512], kc == 0, kc == 7,
                           rd=rt("hT", range(tb * 4, tb * 4 + 4)) + [r_wbuf[s]], wr=[r_ps[b]])
                    consume(m, tb, b)

        def qk_norm(t, b, is_k):
            f = nxt("F", 4)
            cp("act", Ft[f][:], ps[b][:], rd=[r_ps[b]], wr=[r_Ft[f]])
            tt("dve", sq[:], Ft[f][:], Ft[f][:], ALU.mult, rd=[r_Ft[f]], wr=[res("sq")])
            si = nxt("st8", 2)
            P.op("dve", lambda e: e.tensor_reduce(out=st8[si][:], in_=sq[:].rearrange("p (h d) -> p h d", h=8),
                                                  axis=AX.X, op=ALU.add), rd=[res("sq")], wr=[res("st8_%d" % si)])
            act(st8[si][:], st8[si][:], AF.Sqrt, rd=[res("st8_%d" % si)], wr=[res("st8_%d" % si)], scale=1.0 / 64, bias=EPS)
            P.op("dve", lambda e: e.reciprocal(out=st8[si][:], in_=st8[si][:]), rd=[res("st8_%d" % si)], wr=[res("st8_%d" % si)])
            bi = nxt("B", 4)
            f3 = Ft[f][:].rearrange("p (h d) -> p h d", h=8)
            if not is_k:
                tt("dve", Bt[bi][:].rearrange("p (h d) -> p h d", h=8), f3, st8[si][:].unsqueeze(2).to_broadcast([128, 8, 64]),
                   ALU.mult, rd=[r_Ft[f], res("st8_%d" % si)], wr=[r_Bt[bi]])
                return lambda: transposes_to(lambda j: QT2[:, j, t * 128:(t + 1) * 128], Bt[bi], 4, rd=[r_Bt[bi]],
                                             wr=rt("QT2", [t]), scale_ap=gq8[:, 0:1])
            else:
                tt("dve", f3, f3, st8[si][:].unsqueeze(2).to_broadcast([128, 8, 64]), ALU.mult,
                   rd=[r_Ft[f], res("st8_%d" % si)], wr=[r_Ft[f]])
                tt("dve", f3, f3, gkb[:].unsqueeze(1).to_broadcast([128, 8, 64]), ALU.mult,
                   rd=[r_Ft[f], res("gkb")], wr=[r_Ft[f]])
                if t < 4:
                    sq_, half = t // 2, t % 2
                    dst = newk[sq_, l].rearrange("h t d -> t h d")[half * 128:(half + 1) * 128]
                    dma("sp", dst, f3, rd=[r_Ft[f]])
                cp("act", Bt[bi][:], Ft[f][:], rd=[r_Ft[f]], wr=[r_Bt[bi]])
                if t >= 10:
                    r0 = 1024 + (t - 10) * 128
                    cc_dma(l, cc_in[l][r0:r0 + 128, :], Bt[bi][:], [r_Bt[bi]])
                return lambda: transposes_to(lambda j: KT2[:, j, t * 128:(t + 1) * 128], Bt[bi], 4, rd=[r_Bt[bi]],
                                             wr=rt("KT2", [t]))

        sq_slot = nxt("w", 3)
        wload(wbuf[sq_slot][:, :, :], win_view(l, 0, 512), sq_slot)
        sk_slot = nxt("w", 3)
        wload(wbuf[sk_slot][:, :, :], win_view(l, 512, 512), sk_slot)

        def tok_tile(s, t, is_k):
            b = nxt("ps", 4)
            for kc in range(8):
                mm(ps[b][:], hT[:, kc, t * 128:(t + 1) * 128], wbuf[s][:, kc, :], kc == 0, kc == 7,
                   rd=rt("hT", [t]) + [r_wbuf[s]], wr=[r_ps[b]])
            return qk_norm(t, b, is_k)

        deferred = []
        for t in range(NT + 1):
            if t < NT:
                stage0(t)
            newd = []
            if t >= 1:
                newd.append(tok_tile(sq_slot, t - 1, False))
                newd.append(tok_tile(sk_slot, t - 1, True))
            for d in deferred:
                d()
            deferred = newd
        for d in deferred:
            d()

        def v_consume(t, b):
            if t < 4:
                f = nxt("F", 4)
                cp("act", Ft[f][:], ps[b][:], rd=[r_ps[b]], wr=[r_Ft[f]])
                sq_, half = t // 2, t % 2
                dst = newv[sq_, l].rearrange("h t d -> t h d")[half * 128:(half + 1) * 128]
                dma("sp", dst, Ft[f][:].rearrange("p (h d) -> p h d", h=8), rd=[r_Ft[f]])
                cp("dve", V[:, t, :], Ft[f][:], rd=[r_Ft[f]], wr=rt("V", [t]))
            else:
                evac(V[:, t, :], ps[b][:], rd=[r_ps[b]], wr=rt("V", [t]))
            if t >= 10:
                r0 = 1280 + (t - 10) * 128
                cc_dma(l, cc_in[l][r0:r0 + 128, :], V[:, t, :], rt("V", [t]))
        tokmajor_block(1024, 512, v_consume)

        def za_consume(m, tb, b):
            act(zaT[:, m, tb * 512:(tb + 1) * 512], ps[b][:], AF.Silu, rd=[r_ps[b]], wr=rt("zaT", range(tb * 4, tb * 4 + 4)))
        featmajor_block(1536, 4, za_consume)

        def ubzb_consume(m, tb, b):
            if m < 2:
                evac(ubT[:, m, tb * 512:(tb + 1) * 512], ps[b][:], rd=[r_ps[b]], wr=rt("ubT", range(tb * 4, tb * 4 + 4)))
            else:
                act(zbT[:, m - 2, tb * 512:(tb + 1) * 512], ps[b][:], AF.Silu, rd=[r_ps[b]], wr=rt("zbT", range(tb * 4, tb * 4 + 4)))
        featmajor_block(2048, 4, ubzb_consume)

        def uc_consume(t, b):
            evac(UC[:, t, :], ps[b][:, 0:256], rd=[r_ps[b]], wr=rt("UC", [t]))
            if t == 11:
                cc_dma(l, cc_in[l][1536:1664, 0:256], UC[:, t, :], rt("UC", [t]))
                cc_dma(l, cc_in[l][1536:1664, 256:512], UC[:, t, :], rt("UC", [t]))
        tokmajor_block(2560, 256, uc_consume)

        def zc_consume(m, tb, b):
            act(zcT[:, m, tb * 512:(tb + 1) * 512], ps[b][:], AF.Silu, rd=[r_ps[b]], wr=rt("zcT", range(tb * 4, tb * 4 + 4)))
        featmajor_block(2816, 2, zc_consume)

        def ucus_tile(t, dst_ap, wr):
            b = nxt("ps", 4)
            for cb in range(2):
                mm(ps[b][:, cb * 256:(cb + 1) * 256], ubT[:, cb, t * 128:(t + 1) * 128], cs[:], True, True,
                   rd=rt("ubT", [t]) + [res("cs")], wr=[r_ps[b]])
            evac(dst_ap.rearrange("p (s c h) -> p c s h", s=2, c=2), ps[b][:].rearrange("p (c s h) -> p c s h", c=2, s=2),
                 rd=[r_ps[b]], wr=wr)

        for t in range(4, NT):
            bi = nxt("B", 4)
            ucus_tile(t, Bt[bi][:], [r_Bt[bi]])
            cc_dma(l, cc_in[l][(t - 4) * 128:(t - 3) * 128, :], Bt[bi][:], [r_Bt[bi]])

        groups = [[0, 1], [2, 3], [4, 5], [6, 7]]
        P.op("pool", lambda e, l=l: e.collective_compute("AllGather", ALU.bypass, replica_groups=groups,
                                                         ins=[cc_in[l]], outs=[cc_out[l]]),
             rd=list(ccin_all[l]), wr=[res("ccout%d" % l)], kind="cc", inc=1)

        for tl in range(2):
            dma("pool", Kctx[:, tl, :].rearrange("p (h d) -> p h d", h=8),
                cache_k[l][:, tl * 128:(tl + 1) * 128, :].rearrange("h p d -> p h d"), wr=[res("Kctx%d" % tl)])
            dma("pool", Vctx[:, tl, :].rearrange("p (h d) -> p h d", h=8),
                cache_v[l][:, tl * 128:(tl + 1) * 128, :].rearrange("h p d -> p h d"), wr=[res("Vctx%d" % tl)])
        for tl in range(2):
            transposes_to(lambda j, tl=tl: KTc2[:, j, tl * 128:(tl + 1) * 128], Kctx[:, tl, :], 4, rd=[res("Kctx%d" % tl)], wr=[res("KTc2")])

        def attend(j, par, q0, nq, keytiles, zres):
            h0, h1 = par * 64, par * 64 + 64
            om = nxt("om", 2)
            bo, bm = 4 + 2 * om, 5 + 2 * om
            r_o, r_m = r_ps[bo], r_ps[bm]
            flat = []
            for (kt, v, rd, chunks) in keytiles:
                for ch in chunks:
                    flat.append((kt, v, rd, ch))
            pend = []

            def pv(pd):
                bi, v, rd, c0, n, first = pd
                mm(ps[bo][:, c0:c0 + n], v, Bt[bi][:, 0:n], first, False, rd=list(rd) + [r_Bt[bi]], wr=[r_o])
                mm(ps[bm][:, c0:c0 + n], onesb[:], Bt[bi][:, 0:n], first, False, rd=[res("onesb"), r_Bt[bi]], wr=[r_m])

            for i, (kt, v, rd, (c0, n, bias, brd)) in enumerate(flat):
                b = nxt("ps", 4)
                mm(ps[b][:, 0:n], kt, QT2[h0:h1, j, q0 + c0:q0 + c0 + n], True, bias is None,
                   rd=list(rd) + zres["q"], wr=[r_ps[b]])
                if bias is not None:
                    mm(ps[b][:, 0:n], ident[:], bias, False, True, rd=[res("ident")] + brd, wr=[r_ps[b]])
                bi = nxt("B", 4)
                act(Bt[bi][:, 0:n], ps[b][:, 0:n], AF.Exp, rd=[r_ps[b]], wr=[r_Bt[bi]])
                pend.append((bi, v, rd, c0, n, i == 0))
                if len(pend) > 2:
                    pv(pend.pop(0))
            while pend:
                pv(pend.pop(0))
            f1 = nxt("F", 4)
            P.op("dve", lambda e: e.reciprocal(out=Ft[f1][h0:h1, 0:nq], in_=ps[bm][h0:h1, 0:nq]), rd=[r_m], wr=[r_Ft[f1]])
            f2 = nxt("F", 4)
            tt("dve", Ft[f2][h0:h1, 0:nq], ps[bo][h0:h1, 0:nq], Ft[f1][h0:h1, 0:nq], ALU.mult, rd=[r_o, r_Ft[f1]], wr=[r_Ft[f2]])
            tt("dve", zaT[h0:h1, j, q0:q0 + nq], Ft[f2][h0:h1, 0:nq], zaT[h0:h1, j, q0:q0 + nq], ALU.mult,
               rd=[r_Ft[f2]] + zres["z"], wr=zres["z"])

        for sq_ in range(2):
            tl = [2 * sq_, 2 * sq_ + 1]
            for h in range(8):
                j, par = h // 2, h % 2
                kts = []
                for t in tl:
                    kts.append((KT2[par * 64:par * 64 + 64, j, t * 128:(t + 1) * 128], V[:, t, j * 128:(j + 1) * 128],
                                rt("KT2", [t]) + rt("V", [t]), [(0, 256, None, [])]))
                attend(j, par, sq_ * 256, 256, kts, dict(q=rt("QT2", tl), z=rt("zaT", tl)))

        for rk in range(2):
            base = rk * 1664
            dma("sp", Khalo[:, rk * 2:rk * 2 + 2, :], cc_out[l][base + 1024:base + 1280, :].rearrange("(t p) n -> p t n", p=128),
                rd=[res("ccout%d" % l)], wr=[res("Khalo")])
            dma("sp", Vh[:, rk * 2:rk * 2 + 2, :], cc_out[l][base + 1280:base + 1536, :].rearrange("(t p) n -> p t n", p=128),
                rd=[res("ccout%d" % l)], wr=[res("Vh")])
            dma("sp", ucE[:, rk, :], cc_out[l][base + 1536:base + 1664, 0:256], rd=[res("ccout%d" % l)], wr=[res("ucE")])
        for hi in range(4):
            transposes_to(lambda j, hi=hi: KTh2[:, j, hi * 128:(hi + 1) * 128], Khalo[:, hi, :], 4, rd=[res("Khalo")], wr=[res("KTh2")])

        for h in range(8):
            j, par = h // 2, h % 2
            for qh in range(2):
                tb = nxt("tt", 1)
                dma("sp", ttb[tb][:], tt_d[l, h, qh], wr=[res("ttb%d" % tb)])
                kts = []
                for tl in range(2):
                    kts.append((KTc2[par * 64:par * 64 + 64, j, tl * 128:(tl + 1) * 128], Vctx[:, tl, j * 128:(j + 1) * 128],
                                [res("KTc2"), res("Vctx%d" % tl)], [(0, 512, None, [])]))
                for (kind, idx, row0, nrows, off) in NA_SCHED[qh]:
                    c0 = (row0 - qh * 8) * 64
                    n = nrows * 64
                    bias = ttb[tb][:, off * 64:off * 64 + n]
                    if kind == "own":
                        t = 4 + idx
                        kts.append((KT2[par * 64:par * 64 + 64, j, t * 128:(t + 1) * 128], V[:, t, j * 128:(j + 1) * 128],
                                    rt("KT2", [t]) + rt("V", [t]), [(c0, n, bias, [res("ttb%d" % tb)])]))
                    else:
                        kts.append((KTh2[par * 64:par * 64 + 64, j, idx * 128:(idx + 1) * 128], Vh[:, idx, j * 128:(j + 1) * 128],
                                    [res("KTh2"), res("Vh")], [(c0, n, bias, [res("ttb%d" % tb)])]))
                tiles = list(range(4 + qh * 4, 8 + qh * 4))
                attend(j, par, 512 + qh * 512, 512, kts, dict(q=rt("QT2", tiles), z=rt("zaT", tiles)))

        allA2 = rt("QT2", range(NT)) + rt("KT2", range(NT))
        for t in range(4):
            ucus_tile(t, UCUS[:, t, :], allA2 + [res("UCUS")])
        for rk in range(2):
            base = rk * 1664
            dma("sp", UCUS[:, 4 + rk * 8:12 + rk * 8, :], cc_out[l][base:base + 1024, :].rearrange("(t p) n -> p t n", p=128),
                rd=[res("ccout%d" % l)], wr=allA2 + [res("UCUS")])

        def fourier_group(q0, nq, src_tiles, table_fn, zres):
            banks = [nxt("ps", 4) for _ in range(2)]
            nsrc = len(src_tiles)
            for i, g in enumerate(src_tiles):
                tab, trd = table_fn(i)
                for cs_i in range(2):
                    for cb in range(2):
                        mm(ps[banks[cb]][:, 0:nq], UCUS[:, g, cs_i * 256 + cb * 128:cs_i * 256 + (cb + 1) * 128],
                           tab[:, cs_i, :], (i == 0 and cs_i == 0), (i == nsrc - 1 and cs_i == 1),
                           rd=[res("UCUS")] + trd, wr=[r_ps[banks[cb]]])
            fb = []
            for cb in range(2):
                bi = nxt("B", 4)
                evac(Bt[bi][:, 0:nq], ps[banks[cb]][:, 0:nq], rd=[r_ps[banks[cb]]], wr=[r_Bt[bi]])
                fb.append(bi)
            for m in range(2):
                b = nxt("ps", 4)
                for cb in range(2):
                    mm(ps[b][:, 0:nq], wfn[:, cb, m * 128:(m + 1) * 128], Bt[fb[cb]][:, 0:nq], cb == 0, cb == 1,
                       rd=[res("wfn"), r_Bt[fb[cb]]], wr=[r_ps[b]])
                tt("dve", zbT[:, m, q0:q0 + nq], ps[b][:, 0:nq], zbT[:, m, q0:q0 + nq], ALU.mult, rd=[r_ps[b]] + zres, wr=zres)

        for sq_ in range(2):
            fourier_group(sq_ * 256, 256, [2 * sq_, 2 * sq_ + 1],
                          lambda i: (dftp[:, i, :, :], [res("dftp")]), rt("zbT", [2 * sq_, 2 * sq_ + 1]))
        for qh in range(2):
            slabs = {}

            def tab_fn(i, qh=qh):
                di = nxt("ds", 2)
                dma("sp", dslab[di][:, :, 0:512], dfts_d[i, :, :, qh * 512:(qh + 1) * 512], wr=[res("dslab%d" % di)])
                return dslab[di][:, :, 0:512], [res("dslab%d" % di)]
            fourier_group(512 + qh * 512, 512, list(range(4, 20)), tab_fn, rt("zbT", range(4 + qh * 4, 8 + qh * 4)))

        for t in range(NT):
            dma("sp", bndb[:], bnd_d[t], wr=[res("bndb")])
            b = nxt("ps", 4)
            slots = _pool_slots(t)
            for g in range(4):
                for k, (kind, idx) in enumerate(slots):
                    if kind == "own":
                        src, srd = UC[:, idx, g * 64:(g + 1) * 64], rt("UC", [idx])
                    else:
                        src, srd = ucE[:, idx, g * 64:(g + 1) * 64], [res("ucE")]
                    mm(ps[b][:, g * 64:(g + 1) * 64], bndb[:, g, k, :], src, k == 0, k == len(slots) - 1,
                       rd=[res("bndb")] + srd, wr=[r_ps[b]])
            bi = nxt("B", 4)
            evac(Bt[bi][:, 0:256], ps[b][:, 0:256], rd=[r_ps[b]], wr=[r_Bt[bi]])
            b2 = nxt("B", 4)
            transposes_to(lambda c, b2=b2: Bt[b2][:, c * 128:(c + 1) * 128], Bt[bi], 2, rd=[r_Bt[bi]], wr=[r_Bt[b2]])
            for m in range(2):
                b3 = nxt("ps", 4)
                mm(ps[b3][:, 0:128], wpl[:, m, :], Bt[b2][:, m * 128:(m + 1) * 128], True, True,
                   rd=[res("wpl"), r_Bt[b2]], wr=[r_ps[b3]])
                f = nxt("F", 4)
                ts("dve", Ft[f][:, 0:128], ps[b3][:, 0:128], pscT[:, m:m + 1], None, ALU.mult, rd=[r_ps[b3], res("pscT")], wr=[r_Ft[f]])
                tt("dve", zcT[:, m, t * 128:(t + 1) * 128], Ft[f][:, 0:128], zcT[:, m, t * 128:(t + 1) * 128], ALU.mult,
                   rd=[r_Ft[f]] + rt("zcT", [t]), wr=rt("zcT", [t]))

        GA = [(zaT, 4, "zaT", p_a), (zbT, 2, "zbT", p_b), (zcT, 2, "zcT", p_c)]
        koff = [0, 4, 6]
        for nb in range(2):
            for br in range(3):
                gbuf, nk, gname, pw = GA[br]
                dma("pool", pbuf[:, koff[br]:koff[br] + nk, :],
                    pw[l].rearrange("(kc p) n -> p kc n", p=128)[:, :, nb * 512:(nb + 1) * 512],
                    wr=[res("pbuf%d" % br)])
            for br in range(3):
                gbuf, nk, gname, pw = GA[br]
                s = nxt("w", 3)
                wload(wbuf[s][:, :, :], win_view(l, 3072 + br * 1024 + nb * 512, 512), s)
                for t in range(NT):
                    b = nxt("ps", 4)
                    for kc in range(8):
                        mm(ps[b][:], hT[:, kc, t * 128:(t + 1) * 128], wbuf[s][:, kc, :], kc == 0, kc == 7,
                           rd=rt("hT", [t]) + [r_wbuf[s]], wr=[r_ps[b]])
                    f = nxt("F", 4)
                    act(Ft[f][:], ps[b][:], AF.Sigmoid, rd=[r_ps[b]], wr=[r_Ft[f]])
                    b2 = nxt("ps", 4)
                    for kc in range(nk):
                        mm(ps[b2][:], gbuf[:, kc, t * 128:(t + 1) * 128], pbuf[:, koff[br] + kc, :], kc == 0, kc == nk - 1,
                           rd=rt(gname, [t]) + [res("pbuf%d" % br)], wr=[r_ps[b2]])
                    mdst = MRG[:, t, nb * 512:(nb + 1) * 512]
                    mres = [res("MRG_%d_%d" % (t, nb))] + (allA2 + [res("UCUS")] if br == 0 else [])
                    if br == 0:
                        tt("dve", mdst, ps[b2][:], Ft[f][:], ALU.mult, rd=[r_ps[b2], r_Ft[f]], wr=mres)
                    else:
                        tt("dve", Ft[f][:], ps[b2][:], Ft[f][:], ALU.mult, rd=[r_ps[b2], r_Ft[f]], wr=[r_Ft[f]])
                        tt("dve", mdst, mdst, Ft[f][:], ALU.add, rd=[r_Ft[f]] + mres, wr=mres)

        allA3 = rt("zaT", range(NT)) + rt("zbT", range(NT)) + rt("zcT", range(NT))
        for t in range(NT):
            for half in range(2):
                b = nxt("ps", 4)
                for c in range(4):
                    tr(pst[b][:, c * 128:(c + 1) * 128], MRG[:, t, half * 512 + c * 128:half * 512 + (c + 1) * 128],
                       rd=[res("MRG_%d_%d" % (t, half))] + allA2 + [res("UCUS")], wr=[r_pst[b]])
                evac(MT[:, half * 4:half * 4 + 4, t * 128:(t + 1) * 128],
                     pst[b][:, 0:512].rearrange("p (c n) -> p c n", c=4), rd=[r_pst[b]],
                     wr=rt("zaT", [t]) + rt("zbT", [t]) + rt("zcT", [t]))
        for nb in range(2):
            s = nxt("w", 3)
            wload(wbuf[s][:, :, :], w_o[l].rearrange("(kc p) n -> p kc n", p=128)[:, :, nb * 512:(nb + 1) * 512], s)
            for t in range(NT):
                cond = 0 if t < 4 else 1
                b = nxt("ps", 4)
                for kc in range(8):
                    mm(ps[b][:], MT[:, kc, t * 128:(t + 1) * 128], wbuf[s][:, kc, :], kc == 0, kc == 7,
                       rd=rt("zaT", [t]) + rt("zbT", [t]) + rt("zcT", [t]) + [r_wbuf[s]], wr=[r_ps[b]])
                f = nxt("F", 4)
                dma("sp", Ft[f][:], xsrc[t * 128:(t + 1) * 128, nb * 512:(nb + 1) * 512], rd=rt("xdram%d" % l, [t]), wr=[r_Ft[f]])
                f2 = nxt("F", 4)
                tt("dve", Ft[f2][:], ps[b][:], gate_bc[cond][:, nb * 512:(nb + 1) * 512], ALU.mult,
                   rd=[r_ps[b], res("gate_bc%d" % cond)], wr=[r_Ft[f2]])
                tt("dve", Ft[f2][:], Ft[f2][:], Ft[f][:], ALU.add, rd=[r_Ft[f2], r_Ft[f]], wr=[r_Ft[f2]])
                dma("sp", xdst[t * 128:(t + 1) * 128, nb * 512:(nb + 1) * 512], Ft[f2][:], rd=[r_Ft[f2]],
                    wr=rt("xdram%d" % (l + 1), [t]))

    P.emit(nc)
    st.close()
    return nc


def _tok_global(rank, j):
    j = np.asarray(j)
    return j if rank == 0 else 2047 - j


def _bias_tables(rpb, rank):
    out = np.full((2, 8, 2, 128, NBLK * 64), NEG, np.float32)
    for qh in range(2):
        for (kind, idx, row0, nrows, off) in NA_SCHED[qh]:
            if kind == "own":
                kloc = (2 * idx) * 64 + np.arange(128)
                kg = _tok_global(rank, kloc)
                valid_tile = True
            else:
                q_rank, ab = idx // 2, idx % 2
                kloc = 768 + ab * 128 + np.arange(128)
                kg = _tok_global(q_rank, kloc)
                valid_tile = (q_rank != rank)
            for r in range(nrows):
                p = row0 + r
                qg = _tok_global(rank, p * 64 + np.arange(64))
                qr, qc = qg // 64, qg % 64
                kr, kc = kg // 64, kg % 64
                rs = np.clip(qr - 4, 0, 24)
                cst = np.clip(qc - 8, 0, 48)
                ok = ((kr[:, None] >= rs[None, :]) & (kr[:, None] < rs[None, :] + 8) &
                      (kc[:, None] >= cst[None, :]) & (kc[:, None] < cst[None, :] + 16))
                if not valid_tile:
                    ok[:] = False
                ri = np.clip(kr[:, None] - qr[None, :] + 7, 0, 14)
                ci = np.clip(kc[:, None] - qc[None, :] + 15, 0, 30)
                vals = rpb[:, :, ri, ci]
                blk = np.where(ok[None, None], vals, NEG)
                c0 = (off + r) * 64
                out[:, :, qh, :, c0:c0 + 64] = blk
    return out.astype(NPBF)


def _dft_tables(rank):
    g = np.arange(2048)
    gin = np.where(g < 1024, g, 2047 - (g - 1024)).astype(np.float64)
    tout = _tok_global(rank, np.arange(1024)).astype(np.float64)
    ang = 2 * np.pi * np.outer(gin, tout) / 2048.0
    nrm = 1.0 / np.sqrt(2048.0 * 64.0)
    tab = np.stack([np.cos(ang) * nrm, -np.sin(ang) * nrm], axis=1)
    return tab.reshape(16, 128, 2, 1024).astype(NPBF)


def _dft_prompt():
    t = np.arange(256, dtype=np.float64)
    ang = 2 * np.pi * np.outer(t, t) / 256.0
    nrm = 1.0 / np.sqrt(256.0 * 64.0)
    tab = np.stack([np.cos(ang) * nrm, -np.sin(ang) * nrm], axis=1)
    tab = tab.reshape(2, 128, 2, 256).transpose(1, 0, 2, 3)
    return np.ascontiguousarray(tab).astype(NPBF)


def _cs_table():
    c = np.arange(128)
    same = (c[:, None] // 64) == (c[None, :] // 64)
    ang = 2 * np.pi * np.outer(c % 64, c % 64) / 64.0
    cc = np.where(same, np.cos(ang), 0.0)
    sc = np.where(same, np.sin(ang), 0.0)
    return np.concatenate([cc, sc], axis=1).astype(NPBF)


def _pool_w(spos, tpos, L, w, same):
    lo = np.clip(tpos - w // 2, 0, L)
    hi = np.clip(tpos + w // 2, 0, L)
    cnt = (hi - lo).astype(np.float64)
    inw = (spos[:, None] >= lo[None, :]) & (spos[:, None] < hi[None, :])
    m = np.where(inw, 1.0 / cnt[None, :], 0.0) - (spos[:, None] == tpos[None, :]).astype(np.float64)
    return m if same else m * 0.0


def _band_tables(rank):
    out = np.zeros((NT, 128, 4, 4, 128), np.float32)
    wins = (2, 4, 8, 16)
    for i in range(NT):
        for k, (kind, idx) in enumerate(_pool_slots(i)):
            if i < 4:
                tpos = (i % 2) * 128 + np.arange(128)
                spos = (idx % 2) * 128 + np.arange(128)
                L, ok = 256, True
            else:
                tpos = _tok_global(rank, (i - 4) * 128 + np.arange(128))
                L = 2048
                if kind == "own":
                    spos = _tok_global(rank, (idx - 4) * 128 + np.arange(128))
                    ok = True
                else:
                    spos = _tok_global(idx, 896 + np.arange(128))
                    ok = (idx != rank)
            for g, w in enumerate(wins):
                out[i, :, g, k, :] = _pool_w(spos, tpos, L, w, ok)
    return out.astype(NPBF)


_NC_CACHE = {}


def kernel(x_prompt, x_sample, cache_k, cache_v, c, c_ctx, norm_g, w_ada, b_ada, w_in,
           q_norm_g, k_norm_g, rpb, w_fnet, w_pool, pool_scale, p_a, p_b, p_c, w_o):
    f32 = lambda a: np.ascontiguousarray(np.asarray(a, dtype=np.float32))
    x_prompt, x_sample, cache_k, cache_v = f32(x_prompt), f32(x_sample), f32(cache_k), f32(cache_v)
    c, c_ctx, norm_g, w_ada, b_ada, w_in = f32(c), f32(c_ctx), f32(norm_g), f32(w_ada), f32(b_ada), f32(w_in)
    q_norm_g, k_norm_g, rpb, w_fnet, w_pool = f32(q_norm_g), f32(k_norm_g), f32(rpb), f32(w_fnet), f32(w_pool)
    pool_scale, p_a, p_b, p_c, w_o = f32(pool_scale), f32(p_a), f32(p_b), f32(p_c), f32(w_o)

    if "nc" not in _NC_CACHE:
        _NC_CACHE["nc"] = build_nc()
    nc = _NC_CACHE["nc"]

    b_adaT = np.ascontiguousarray(b_ada.reshape(2, 24, 128).transpose(0, 2, 1))
    norm_gT = np.ascontiguousarray(norm_g.reshape(2, 8, 128).transpose(0, 2, 1))
    gqT = np.ascontiguousarray(np.concatenate([q_norm_g, q_norm_g], axis=1)[:, :, None])
    w_poolbd = np.zeros((2, 2, 128, 128), np.float32)
    for l in range(2):
        for g in range(4):
            cb, o = g // 2, (g % 2) * 64
            w_poolbd[l, cb, o:o + 64, o:o + 64] = w_pool[l, g]
    pool_scaleT = np.ascontiguousarray(pool_scale.reshape(2, 2, 128).transpose(0, 2, 1))
    ident = np.eye(128, dtype=np.float32)
    tabs = {}
    for rank in range(2):
        tabs[rank] = dict(tt=_bias_tables(rpb, rank), dfts=_dft_tables(rank), bnd=_band_tables(rank))
    dftp = _dft_prompt()
    cs = _cs_table()

    in_maps = []
    for core in range(8):
        b, rank = core // 2, core % 2
        xs = x_sample[b, 0:1024] if rank == 0 else x_sample[b, 1024:2048][::-1]
        xin = np.ascontiguousarray(np.concatenate([x_prompt[2 * core], x_prompt[2 * core + 1], xs], axis=0))
        cond = np.stack([c_ctx, c[b]], axis=0)
        condT = np.ascontiguousarray(cond.reshape(2, 8, 128).transpose(2, 1, 0).reshape(128, 16))
        in_maps.append(dict(
            xin=xin, condT=condT, w_ada=w_ada, b_adaT=b_adaT, norm_gT=norm_gT, w_in=w_in, gqT=gqT,
            k_norm_g=k_norm_g, cache_k=np.ascontiguousarray(cache_k[b]), cache_v=np.ascontiguousarray(cache_v[b]),
            w_fnet=w_fnet, w_poolbd=w_poolbd, pool_scaleT=pool_scaleT, p_a=p_a, p_b=p_b, p_c=p_c, w_o=w_o,
            ident=ident.astype(NPBF), identf=ident, tt=tabs[rank]["tt"], dfts=tabs[rank]["dfts"], dftp=dftp,
            cs=cs, bnd=tabs[rank]["bnd"]))

    res = run_bass_kernel_spmd(nc, in_maps, core_ids=list(range(8)))
    y_prompt = np.zeros((16, 256, D), np.float32)
    y_sample = np.zeros((4, 2048, D), np.float32)
    new_k = np.zeros((16, 2, 8, 256, 64), np.float32)
    new_v = np.zeros((16, 2, 8, 256, 64), np.float32)
    for core in range(8):
        r = res.results[core]
        b, rank = core // 2, core % 2
        yo = np.asarray(r["yout"], dtype=np.float32)
        y_prompt[2 * core] = yo[0:256]
        y_prompt[2 * core + 1] = yo[256:512]
        if rank == 0:
            y_sample[b, 0:1024] = yo[512:1536]
        else:
            y_sample[b, 1024:2048] = yo[512:1536][::-1]
        nk = np.asarray(r["newk"], dtype=np.float32)
        nv = np.asarray(r["newv"], dtype=np.float32)
        new_k[2 * core], new_k[2 * core + 1] = nk[0], nk[1]
        new_v[2 * core], new_v[2 * core + 1] = nv[0], nv[1]
    return (y_prompt, y_sample, new_k, new_v)
```
